# Optimizing a Trainium2 kernel written in Bass

```python
import jax, jax.numpy as jnp
from jax import lax
import numpy as np

D_MODEL = 1024
BATCH = 8
SEQ = 2048
DEPTH = 1

D_MIX = D_MODEL
N_DN_HEADS = 4
DN_HEAD_DIM = 128
D_DN = N_DN_HEADS * DN_HEAD_DIM
D_SC = D_MIX - D_DN
N_SC_GROUPS = 4
DN_CONV = 4
SC_CONV = 3
CHUNK = 64
D_FF = 4 * D_MODEL
EPS = 1e-6
D_IN = 4 * D_DN + 2 * N_DN_HEADS + 3 * D_SC
SPLIT_OFFSETS = (3 * D_DN, 4 * D_DN, 4 * D_DN + N_DN_HEADS, 4 * D_DN + 2 * N_DN_HEADS,
                 4 * D_DN + 2 * N_DN_HEADS + D_SC, 4 * D_DN + 2 * N_DN_HEADS + 2 * D_SC)

kernel_name = "hymba_deltanet_shortconv_adaln_sandwich"


def rmsnorm(x, w):
    xf = x.astype(jnp.float32)
    y = xf * lax.rsqrt(jnp.mean(xf * xf, axis=-1, keepdims=True) + EPS)
    return (y * w.astype(jnp.float32)).astype(x.dtype)


def l2norm(t):
    return t * lax.rsqrt(jnp.sum(t * t, axis=-1, keepdims=True) + EPS)


def causal_depthwise_conv(x, w):
    width = w.shape[0]
    return lax.conv_general_dilated(
        x, w[:, None, :].astype(x.dtype), window_strides=(1,), padding=[(width - 1, 0)],
        dimension_numbers=('NWC', 'WIO', 'NWC'), feature_group_count=x.shape[-1])


def gated_delta_rule_chunked(q, k, v, g, beta):
    bsz, seq, nh, dh = q.shape
    n = seq // CHUNK

    def chunks(t):
        t = t.reshape((bsz, n, CHUNK, nh) + t.shape[3:])
        return jnp.moveaxis(jnp.moveaxis(t, 1, 0), 3, 2)

    q = chunks(q * (dh ** -0.5))
    k = chunks(k)
    v = chunks(v)
    beta = chunks(beta)
    g_cum = jnp.cumsum(chunks(g), axis=-1)

    tri_incl = jnp.tril(jnp.ones((CHUNK, CHUNK), dtype=bool))
    tri_strict = jnp.tril(jnp.ones((CHUNK, CHUNK), dtype=bool), -1)
    diff = g_cum[..., :, None] - g_cum[..., None, :]
    decay = jnp.exp(jnp.where(tri_incl, diff, -jnp.inf))

    k_beta = k * beta[..., None]
    v_beta = v * beta[..., None]
    m = jnp.where(tri_strict, jnp.einsum('nbhid,nbhjd->nbhij', k_beta, k) * decay, 0.0)
    eye = jnp.eye(CHUNK, dtype=m.dtype)
    t_inv = lax.linalg.triangular_solve(m + eye, jnp.broadcast_to(eye, m.shape),
                                        left_side=True, lower=True, unit_diagonal=True)
    u = jnp.einsum('nbhij,nbhjd->nbhid', t_inv, v_beta)
    w = jnp.einsum('nbhij,nbhjd->nbhid', t_inv, k_beta * jnp.exp(g_cum)[..., None])
    attn = jnp.einsum('nbhid,nbhjd->nbhij', q, k) * decay

    def step(state, inp):
        q_c, k_c, u_c, w_c, a_c, g_c = inp
        v_new = u_c - jnp.einsum('bhcd,bhde->bhce', w_c, state)
        o_c = (jnp.einsum('bhcd,bhde->bhce', q_c * jnp.exp(g_c)[..., None], state)
               + jnp.einsum('bhij,bhje->bhie', a_c, v_new))
        g_last = g_c[..., -1]
        k_dec = k_c * jnp.exp(g_last[..., None] - g_c)[..., None]
        state = (state * jnp.exp(g_last)[..., None, None]
                 + jnp.einsum('bhcd,bhce->bhde', k_dec, v_new))
        return state, o_c

    s0 = jnp.zeros((bsz, nh, dh, dh), dtype=q.dtype)
    _, o = lax.scan(step, s0, (q, k, u, w, attn, g_cum))
    o = jnp.moveaxis(jnp.moveaxis(o, 2, 3), 0, 1)
    return o.reshape(bsz, seq, nh, dh)


def token_mixer(h, w_in, dn_conv_w, dn_a_log, dn_dt_bias, dn_norm_w, sc_conv_w, w_out):
    bsz, seq, _ = h.shape
    proj = h @ w_in
    qkv, z, a, b, s_b, s_c, s_h = jnp.split(proj, SPLIT_OFFSETS, axis=-1)

    qkv = jax.nn.silu(causal_depthwise_conv(qkv, dn_conv_w)).astype(jnp.float32)
    q, k, v = [t.reshape(bsz, seq, N_DN_HEADS, DN_HEAD_DIM) for t in jnp.split(qkv, 3, axis=-1)]
    q, k = l2norm(q), l2norm(k)
    g = -jnp.exp(dn_a_log.astype(jnp.float32)) * jax.nn.softplus(
        a.astype(jnp.float32) + dn_dt_bias.astype(jnp.float32))
    beta = jax.nn.sigmoid(b.astype(jnp.float32))
    o = gated_delta_rule_chunked(q, k, v, g, beta)
    zg = jax.nn.silu(z.astype(jnp.float32).reshape(bsz, seq, N_DN_HEADS, DN_HEAD_DIM))
    o_dn = (rmsnorm(o, dn_norm_w) * zg).astype(h.dtype).reshape(bsz, seq, D_DN)

    y_sc = s_b * causal_depthwise_conv(s_c * s_h, sc_conv_w)

    return jnp.concatenate([o_dn, y_sc], axis=-1) @ w_out


def setup_inputs(seed: int = 0) -> dict:
    key = jax.random.key(seed)
    ks = jax.random.split(key, 20)
    L, D = DEPTH, D_MODEL

    def nrm(k, shape, scale):
        return jax.random.normal(k, shape, jnp.float32) * scale

    def gain(k, shape):
        return 1.0 + 0.02 * jax.random.normal(k, shape, jnp.float32)

    dt = jnp.exp(jax.random.uniform(ks[9], (L, N_DN_HEADS), jnp.float32,
                                    float(np.log(1e-3)), float(np.log(1e-1))))
    return {
        "x": jax.random.normal(ks[0], (BATCH, SEQ, D), jnp.float32),
        "c": jax.random.normal(ks[1], (BATCH, D), jnp.float32),
        "w_ada": nrm(ks[2], (L, D, 6 * D), 0.5 * D ** -0.5),
        "b_ada": nrm(ks[3], (L, 6 * D), 0.01),
        "pre_mix_norm_w": gain(ks[4], (L, D)),
        "post_mix_norm_w": gain(ks[5], (L, D)),
        "w_in": nrm(ks[6], (L, D, D_IN), D ** -0.5),
        "dn_conv_w": nrm(ks[7], (L, DN_CONV, 3 * D_DN), DN_CONV ** -0.5),
        "dn_a_log": jnp.log(jax.random.uniform(ks[8], (L, N_DN_HEADS), jnp.float32, 1.0, 16.0)),
        "dn_dt_bias": dt + jnp.log(-jnp.expm1(-dt)),
        "dn_norm_w": gain(ks[10], (L, DN_HEAD_DIM)),
        "sc_conv_w": nrm(ks[11], (L, SC_CONV, D_SC), SC_CONV ** -0.5),
        "w_out": nrm(ks[12], (L, D_MIX, D), D_MIX ** -0.5),
        "pre_ffn_norm_w": gain(ks[13], (L, D)),
        "post_ffn_norm_w": gain(ks[14], (L, D)),
        "w_ff1": nrm(ks[15], (L, D, D_FF), D ** -0.5),
        "w_ff2": nrm(ks[16], (L, D_FF, D), D_FF ** -0.5),
    }


def reference(x, c, w_ada, b_ada, pre_mix_norm_w, post_mix_norm_w, w_in, dn_conv_w,
              dn_a_log, dn_dt_bias, dn_norm_w, sc_conv_w, w_out, pre_ffn_norm_w,
              post_ffn_norm_w, w_ff1, w_ff2):
    c_act = jax.nn.silu(c)
    for l in range(DEPTH):
        mod = (c_act @ w_ada[l] + b_ada[l])[:, None, :]
        sh_m, sc_m, g_m, sh_f, sc_f, g_f = jnp.split(mod, 6, axis=-1)

        h = rmsnorm(x, pre_mix_norm_w[l]) * (1.0 + sc_m) + sh_m
        y = token_mixer(h, w_in[l], dn_conv_w[l], dn_a_log[l], dn_dt_bias[l],
                        dn_norm_w[l], sc_conv_w[l], w_out[l])
        x = x + g_m * rmsnorm(y, post_mix_norm_w[l])

        h = rmsnorm(x, pre_ffn_norm_w[l]) * (1.0 + sc_f) + sh_f
        y = jnp.square(jax.nn.relu(h @ w_ff1[l])) @ w_ff2[l]
        x = x + g_f * rmsnorm(y, post_ffn_norm_w[l])
    return x
```

```python
import bisect
import contextlib
import math

import numpy as np
import ml_dtypes
import concourse.bass as bass
import concourse.mybir as mybir
from concourse.bass_utils import run_bass_kernel_spmd

F32 = mybir.dt.float32
BF16 = mybir.dt.bfloat16
ALU = mybir.AluOpType
AF = mybir.ActivationFunctionType

NCORES = 8
SEQ = 2048
D = 1024
DIN = 3592
NH = 4
DH = 128
DFF = 4096
EPS = 1e-6
NT = SEQ // 128
ST = 2
NS = NT // ST
STOK = ST * 128
BIG = 30000.0


class Buf:
    __slots__ = ("name", "w", "r", "bank", "dead")

    def __init__(self, name="", bank=False):
        self.name = name
        self.w = None
        self.r = []
        self.dead = False
        self.bank = bank


def alias_init(newbufs, oldbufs):
    deps = []
    for o in oldbufs:
        if o.w is not None:
            deps.append(o.w)
        deps.extend(o.r)
    best = {}
    for d in deps:
        k = d[:2] if d[0] == "e" else d[:3]
        if k not in best or d[-1] > best[k][-1]:
            best[k] = d
    deps = list(best.values())
    for n in newbufs:
        n.r = list(deps) + n.r


class Sched:
    ENGS = ("pe", "act", "dve", "pool", "sp")
    NDMA = 12

    def __init__(self, nc):
        self.nc = nc
        self.ops = {e: [] for e in self.ENGS}
        self.last_sig = {e: -1 for e in self.ENGS}
        self.seen = {e: {} for e in self.ENGS}
        self.dma_cnt = {}
        self.dma_rr = {e: 0 for e in self.ENGS}
        self.last_compute = {e: -1 for e in self.ENGS}

    def _add_dep(self, eng, rec, dep):
        if dep is None:
            return
        if dep[0] == "e":
            _, e2, o2 = dep
            if e2 == eng and eng == "pe":
                return
            if self.seen[eng].get(("e", e2), -1) >= o2:
                return
            self.seen[eng][("e", e2)] = o2
            if self.last_sig[e2] < o2:
                last = self.ops[e2][self.last_compute[e2]]
                assert last["ord"] >= o2 and last["fn"] is not None
                last["signal"] = True
                self.last_sig[e2] = last["ord"]
            rec["deps"].append(dep)
        else:
            _, q, k, val = dep
            if self.seen[eng].get(("d", q, k), -1) >= val:
                return
            self.seen[eng][("d", q, k)] = val
            rec["deps"].append(dep)

    def _collect(self, eng, rec, reads, writes):
        for b in list(reads) + list(writes):
            assert not b.dead, ("use of a re-taken ring buffer", b.name)
        for b in reads:
            self._add_dep(eng, rec, b.w)
            if b.bank:
                for r in b.r:
                    if r[0] == "e" and r[1] != eng:
                        self._add_dep(eng, rec, r)
        for b in writes:
            self._add_dep(eng, rec, b.w)
            for r in b.r:
                self._add_dep(eng, rec, r)

    @staticmethod
    def _commit(key, reads, writes):
        for b in reads:
            b.r.append(key)
            if len(b.r) > 64:
                b.r = b.r[-64:] if False else b.r
        for b in writes:
            b.w = key
            b.r = []

    def op(self, eng, fn, reads=(), writes=()):
        rec = dict(ord=len(self.ops[eng]), deps=[], fn=fn, signal=False, dma=None)
        self._collect(eng, rec, reads, writes)
        self.ops[eng].append(rec)
        self.last_compute[eng] = rec["ord"]
        self._commit(("e", eng, rec["ord"]), reads, writes)
        return rec

    def dma(self, queue, out, in_, reads=(), writes=()):
        k = self.dma_rr[queue]
        self.dma_rr[queue] = (k + 1) % self.NDMA
        n = self.dma_cnt.get((queue, k), 0)
        rec = dict(ord=len(self.ops[queue]), deps=[], fn=None, signal=False, dma=(k, out, in_))
        if n > 0:
            self._add_dep(queue, rec, ("d", queue, k, 16 * n))
        self._collect(queue, rec, reads, writes)
        self.ops[queue].append(rec)
        self.dma_cnt[(queue, k)] = n + 1
        self._commit(("d", queue, k, 16 * (n + 1)), reads, writes)
        return rec

    def emit(self, final_bufs=()):
        nc = self.nc
        with contextlib.ExitStack() as st:
            esem = {e: st.enter_context(nc.semaphore("s_" + e)) for e in self.ENGS}
            dsem = {}
            for (q, k) in sorted(self.dma_cnt):
                dsem[(q, k)] = st.enter_context(nc.semaphore("d_%s_%d" % (q, k)))
            if final_bufs:
                rec = dict(ord=len(self.ops["sp"]), deps=[], fn=None, signal=False, dma=None)
                for b in final_bufs:
                    self._add_dep("sp", rec, b.w)
                    for r in b.r:
                        self._add_dep("sp", rec, r)
                self.ops["sp"].append(rec)
            sigs = {e: [r["ord"] for r in self.ops[e] if r["signal"]] for e in self.ENGS}

            def resolve(dep):
                if dep[0] == "e":
                    _, e2, o2 = dep
                    j = bisect.bisect_left(sigs[e2], o2)
                    assert j < len(sigs[e2]), dep
                    return esem[e2], j + 1
                _, q, k, val = dep
                return dsem[(q, k)], val

            def make(ename):
                def f(eng):
                    for rec in self.ops[ename]:
                        for dep in rec["deps"]:
                            s, v = resolve(dep)
                            eng.wait_ge(s, v)
                        if rec["dma"] is not None:
                            k, out, in_ = rec["dma"]
                            eng.dma_start(out=out, in_=in_).then_inc(dsem[(ename, k)], 16)
                        elif rec["fn"] is not None:
                            ins = rec["fn"](eng)
                            if rec["signal"]:
                                ins.then_inc(esem[ename], 1)
                return f

            with nc.Block() as block:
                block.tensor(make("pe"))
                block.scalar(make("act"))
                block.vector(make("dve"))
                block.gpsimd(make("pool"))
                block.sync(make("sp"))
        self.stats = {e: (len(self.ops[e]), len(sigs[e])) for e in self.ENGS}


class Ring:
    def __init__(self, aps, name, bank=False):
        self.items = [(ap, Buf("%s%d" % (name, i), bank)) for i, ap in enumerate(aps)]
        self.i = 0

    def take(self):
        ap, old = self.items[self.i]
        new = Buf(old.name, old.bank)
        new.w, new.r = old.w, old.r
        old.dead = True
        self.items[self.i] = (ap, new)
        self.i = (self.i + 1) % len(self.items)
        return ap, new

    def bufs(self):
        return [b for _, b in self.items]


class Arena:
    def __init__(self, t, base, size):
        self.t, self.base, self.size, self.cur = t, base, size, 0

    def alloc(self, nbytes, dt=BF16):
        nbytes = (nbytes + 63) // 64 * 64
        assert self.cur + nbytes <= self.size, ("arena overflow", self.cur, nbytes, self.size)
        off = self.base + self.cur
        self.cur += nbytes
        ap = self.t[:, off // 2:(off + nbytes) // 2]
        if dt == F32:
            ap = ap.bitcast(F32)
        return ap


class _Stop(Exception):
    pass


def build_program(dbg=False, stage=None):
    nc = bass.Bass("TRN2", target_bir_lowering=False)
    S = Sched(nc)

    def checkpoint(n):
        if stage == n:
            S.emit(final_bufs=[OUTB])
            raise _Stop()

    def din(name, shape, dt=F32):
        return nc.dram_tensor(name, list(shape), dt, kind="ExternalInput").ap()

    x_d = din("x", [SEQ, D])
    ccol_d = din("ccol", [128, 8])
    wada_d = din("w_ada", [D, 6 * D]).rearrange("(kc p) n -> p kc n", p=128)
    badac_d = din("b_ada_col", [128, 6, 8])
    bada_d = din("b_ada", [6 * D])
    wprec_d = din("w_pre_col", [128, 8])
    wpreffnc_d = din("w_preffn_col", [128, 8])
    wpost_d = din("w_post", [D])
    wpostffn_d = din("w_postffn", [D])
    win_d = din("w_in", [D, DIN]).rearrange("(kc p) n -> p kc n", p=128)
    cw_d = din("dn_conv_col", [128, 12, 4])
    alog_d = din("dn_a_log", [NH])
    dtb_d = din("dn_dt_bias", [NH])
    nw_d = din("dn_norm_w", [DH]).rearrange("(p o) -> p o", o=1)
    scw_d = din("sc_conv_col", [128, 4, 3])
    wout_d = din("w_out", [D, D]).rearrange("(kc p) n -> p kc n", p=128)
    wff1_d = din("w_ff1", [D, DFF]).rearrange("(kc p) n -> p kc n", p=128)
    wff2_d = din("w_ff2", [DFF, D]).rearrange("(fc p) n -> p fc n", p=128)
    identb_d = din("ident_bf", [128, 128], BF16)
    tri_d = din("tri", [128, 128])
    pms_d = din("pm_strict", [128, 128])
    pmt_d = din("pm_t", [128, 128])
    lvl_d = din("lvlmaskT", [128, 7, 128], BF16)
    out_d = nc.dram_tensor("out", [SEQ, D], F32, kind="ExternalOutput").ap()
    dbg_outs = {}
    OUTB = Buf("out")

    def MM(out, lhsT, rhs, start=True, stop=True, rd=(), wr=()):
        S.op("pe", lambda e: e.matmul(out, lhsT=lhsT, rhs=rhs, start=start, stop=stop), rd, wr)

    def TR(out, in_, ident, rd=(), wr=()):
        S.op("pe", lambda e: e.transpose(out, in_, ident), rd, wr)

    def ACT(out, in_, func, bias=None, scale=None, accum=None, rd=(), wr=()):
        kw = {}
        if bias is not None:
            kw["bias"] = bias
        if scale is not None:
            kw["scale"] = scale
        if accum is not None:
            kw["accum_out"] = accum
        S.op("act", lambda e: e.activation(out=out, in_=in_, func=func, **kw), rd, wr)

    def TT(eng, out, in0, in1, op, rd=(), wr=()):
        S.op(eng, lambda e: e.tensor_tensor(out=out, in0=in0, in1=in1, op=op), rd, wr)

    def TS(eng, out, in0, s1, op0, s2=None, op1=None, rd=(), wr=()):
        if op1 is None:
            S.op(eng, lambda e: e.tensor_scalar(out=out, in0=in0, scalar1=s1, scalar2=None, op0=op0), rd, wr)
        else:
            S.op(eng, lambda e: e.tensor_scalar(out=out, in0=in0, scalar1=s1, scalar2=s2, op0=op0, op1=op1), rd, wr)

    def STT(out, in0, scalar, in1, op0, op1, rd=(), wr=()):
        S.op("dve", lambda e: e.scalar_tensor_tensor(out=out, in0=in0, scalar=scalar, in1=in1, op0=op0, op1=op1), rd, wr)

    def CP(eng, out, in_, rd=(), wr=()):
        if eng == "act":
            S.op("act", lambda e: e.copy(out=out, in_=in_), rd, wr)
        else:
            S.op(eng, lambda e: e.tensor_copy(out=out, in_=in_), rd, wr)

    def MEMSET(eng, ap, val, wr=()):
        S.op(eng, lambda e: e.memset(ap, val), (), wr)

    def RECIP(out, in_, rd=(), wr=()):
        S.op("dve", lambda e: e.reciprocal(out=out, in_=in_), rd, wr)

    def DMA(q, out, in_, rd=(), wr=()):
        S.dma(q, out, in_, rd, wr)

    def dump(name, ap, bufs, dt=F32):
        if not dbg:
            return
        shp = list(ap.shape)
        d = nc.dram_tensor("dbg_" + name, shp, dt, kind="ExternalOutput").ap()
        dbg_outs[name] = d
        DMA("sp", d, ap, rd=bufs, wr=[OUTB])

    try:
        with contextlib.ExitStack() as st:
            TOTAL = 212736
            arena_t = st.enter_context(nc.sbuf_tensor("arena", [128, TOTAL // 2], BF16))
            P_SZ, R1_SZ, R3_SZ, A_SZ = 16 * 1024, 64 * 1024, 32 * 1024, 64 * 1024
            BS_SZ = TOTAL - (P_SZ + R1_SZ + R3_SZ + A_SZ)
            AP_ = Arena(arena_t, 0, P_SZ)
            R1 = arena_t[:, P_SZ // 2:(P_SZ + R1_SZ) // 2]
            WIN_BYTES = 8 * DIN * 2
            AA2 = Arena(arena_t, P_SZ + WIN_BYTES, R1_SZ - WIN_BYTES)
            R3 = arena_t[:, (P_SZ + R1_SZ) // 2:(P_SZ + R1_SZ + R3_SZ) // 2]
            A_BASE = P_SZ + R1_SZ + R3_SZ
            RA = arena_t[:, A_BASE // 2:(A_BASE + A_SZ) // 2]
            AA = Arena(arena_t, A_BASE, A_SZ)
            BS_BASE = A_BASE + A_SZ
            AB = Arena(arena_t, BS_BASE, BS_SZ - 1024)

            banks_bf = [st.enter_context(nc.psum_tensor("bank%d" % i, [128, 1024], BF16))[:, :] for i in range(8)]
            pbanks = [b.bitcast(F32) for b in banks_bf[0:6]]
            ptbs = banks_bf[6:8]
            PT = Ring([t[:, :] for t in ptbs], "pt", bank=True)
            PB = Ring([pb[:, :] for pb in pbanks[0:2]], "pbp", bank=True)
            PD = Ring([pb[:, :] for pb in pbanks[2:6]], "pbd", bank=True)

            def pconst(nbytes, dt, name):
                return AP_.alloc(nbytes, dt), Buf(name)

            ident, identB = pconst(256, BF16, "ident")
            tri, triB = pconst(512, F32, "tri")
            onesb, onesbB = pconst(256, BF16, "onesb")
            onesf, onesfB = pconst(512, F32, "onesf")
            pms, pmsB = pconst(512, F32, "pms")
            pmt, pmtB = pconst(512, F32, "pmt")
            lvlT, lvlTB = pconst(7 * 256, BF16, "lvlT")
            lvlT = lvlT.rearrange("p (k n) -> p k n", k=7)
            nwbc, nwbcB = pconst(512, F32, "nwbc")
            smallc, smallB = pconst(64 * 4, F32, "smallc")
            dtb = smallc[:, 0:4]
            alog = smallc[:, 4:8]
            negA = smallc[:, 8:12]
            ccol = smallc[:, 16:24]
            cact = smallc[:, 24:32]
            cwt, cwB = pconst(12 * 4 * 4, F32, "cw")
            cwt = cwt.rearrange("p (c j) -> p c j", c=12)
            scwt, scwB = pconst(4 * 3 * 4, F32, "scw")
            scwt = scwt[:, 0:12].rearrange("p (c j) -> p c j", c=4)
            modc, modcB = pconst(4 * 8 * 4, F32, "modc")
            modc = modc.rearrange("p (v k) -> p v k", v=4)
            badac, badacB = pconst(6 * 8 * 4, F32, "badac")
            badac = badac.rearrange("p (v k) -> p v k", v=6)
            wnc, wncB = pconst(2 * 8 * 4, F32, "wnc")
            wnc = wnc.rearrange("p (v k) -> p v k", v=2)
            Gm, GmB = pconst(4096, F32, "Gm")
            Gf, GfB = pconst(4096, F32, "Gf")
            cactb, cactbB = pconst(8 * 2, BF16, "cactb")
            cactb = cactb[:, 0:8]
            crep_ap, crepB = pconst(2048, BF16, "crep")
            crep_ap = crep_ap.rearrange("p (k m) -> p k m", k=8)
            ysq, ysqB = pconst(128, F32, "ysq")
            print("persistent arena used", AP_.cur, "of", AP_.size)

            DMA("sp", ident, identb_d, wr=[identB])
            DMA("sp", tri, tri_d, wr=[triB])
            DMA("sp", pms, pms_d, wr=[pmsB])
            DMA("sp", pmt, pmt_d, wr=[pmtB])
            DMA("sp", lvlT, lvl_d, wr=[lvlTB])
            DMA("sp", nwbc[:, 0:1], nw_d, wr=[nwbcB])
            DMA("sp", dtb, dtb_d.partition_broadcast(128), wr=[smallB])
            DMA("sp", alog, alog_d.partition_broadcast(128), wr=[smallB])
            DMA("sp", ccol, ccol_d, wr=[smallB])
            DMA("sp", cwt, cw_d, wr=[cwB])
            DMA("sp", scwt, scw_d, wr=[scwB])
            DMA("sp", badac, badac_d, wr=[badacB])
            DMA("sp", wnc[:, 0, :], wprec_d, wr=[wncB])
            DMA("sp", wnc[:, 1, :], wpreffnc_d, wr=[wncB])
            MEMSET("dve", onesb, 1.0, wr=[onesbB])
            MEMSET("dve", onesf, 1.0, wr=[onesfB])
            ACT(negA, alog, AF.Exp, rd=[smallB], wr=[smallB])
            TS("dve", negA, negA, -1.0, ALU.mult, rd=[smallB], wr=[smallB])
            ACT(cact, ccol, AF.Exp, scale=-1.0, rd=[smallB], wr=[smallB])
            TS("dve", cact, cact, 1.0, ALU.add, rd=[smallB], wr=[smallB])
            RECIP(cact, cact, rd=[smallB], wr=[smallB])
            TT("dve", cact, ccol, cact, ALU.mult, rd=[smallB], wr=[smallB])
            CP("dve", cactb, cact, rd=[smallB], wr=[cactbB])
            dump("cact", cact, [smallB])
            checkpoint(11)

            win_s = R1[:, 0:8 * DIN].rearrange("p (k n) -> p k n", k=8)
            WIN_SL = [(0, 512), (512, 1024), (1024, 1536), (1536, 2056), (2056, 2568), (2568, 3080), (3080, 3592)]
            winB = [Buf("win%d" % i) for i in range(len(WIN_SL))]

            def win_buf(c0):
                for i, (a, b) in enumerate(WIN_SL):
                    if a <= c0 < b:
                        return winB[i]
                raise AssertionError

            wout_s = AB.alloc(16 * 1024).rearrange("p (k n) -> p k n", k=8)
            woutB = [Buf("wout0"), Buf("wout1")]

            wst = [R3[:, i * 4096:(i + 1) * 4096].rearrange("p (k n) -> p k n", k=8) for i in range(2)]
            wstB = [Buf("wst0"), Buf("wst1")]
            wst2 = arena_t[:, BS_BASE // 2:BS_BASE // 2 + 4096].rearrange("p (k n) -> p k n", k=8)
            wst2B = Buf("wst2")
            bst_ap = AB.alloc(4096, F32)
            bst = [bst_ap[:, 0:512], bst_ap[:, 512:1024]]
            bstB = [Buf("bst0"), Buf("bst1")]
            junk0 = Arena(arena_t, BS_BASE + BS_SZ - 1024, 1024).alloc(1024)
            junk0B = Buf("junk0")
            R3_stage_bufs = wstB
            B0_bufs = [junk0B]

            modcBf = Buf("modc_f")

            def ada_stage(j):
                return (wst[j % 2], wstB[j % 2]) if j < 4 else (wst2, wst2B)

            def ada_dma(j):
                stg, stgB = ada_stage(j)
                DMA("pool", stg, wada_d[:, :, j * 512:(j + 1) * 512], wr=[stgB])

            def ada_slice(j, ps, psB, dma=True):
                vec, half = j // 2, j % 2
                sl = slice(j * 512, (j + 1) * 512)
                stg, stgB = ada_stage(j)
                if dma:
                    ada_dma(j)
                if vec in (2, 5):
                    for kc in range(8):
                        MM(ps, crep_ap[:, kc, :], stg[:, kc, :], start=(kc == 0), stop=(kc == 7), rd=[crepB, stgB], wr=[psB])
                    b0, b0B = bst[0], bstB[0]
                    b1, b1B = bst[1], bstB[1]
                    DMA("sp", b0, bada_d[sl].partition_broadcast(128), wr=[b0B])
                    wsrc = wpost_d if vec == 2 else wpostffn_d
                    DMA("sp", b1, wsrc[half * 512:(half + 1) * 512].partition_broadcast(128), wr=[b1B])
                    G, GB = (Gm, GmB) if vec == 2 else (Gf, GfB)
                    Gs = G[:, half * 512:(half + 1) * 512]
                    TT("dve", Gs, ps, b0, ALU.add, rd=[psB, b0B], wr=[GB])
                    TT("dve", Gs, Gs, b1, ALU.mult, rd=[b1B, GB], wr=[GB])
                else:
                    for fb in range(4):
                        for kc in range(8):
                            MM(ps[:, fb:fb + 1], stg[:, kc, fb * 128:(fb + 1) * 128], cactb[:, kc:kc + 1],
                               start=(kc == 0), stop=(kc == 7), rd=[stgB, cactbB], wr=[psB])
                    ci = {0: 1, 1: 0, 3: 3, 4: 2}[vec]
                    mB = modcB if ci < 2 else modcBf
                    dst = modc[:, ci, half * 4:(half + 1) * 4]
                    TT("dve", dst, ps[:, 0:4], badac[:, vec, half * 4:(half + 1) * 4], ALU.add, rd=[psB, badacB], wr=[mB])
                    if vec in (1, 4):
                        wn = wnc[:, 0 if vec == 1 else 1, half * 4:(half + 1) * 4]
                        STT(dst, dst, 1.0, wn, ALU.add, ALU.mult, rd=[mB, wncB], wr=[mB])

            for kc in range(8):
                TS("dve", crep_ap[:, kc, :], onesb, cact[:, kc:kc + 1], ALU.mult, rd=[onesbB, smallB], wr=[crepB])


            for j in (0, 1, 2, 3):
                ps, psB = PB.take()
                ada_slice(j, ps, psB)
            dump("modc_a", modc.rearrange("p v k -> p (v k)"), [modcB])
            checkpoint(12)
            for i, (a, b) in enumerate(WIN_SL):
                DMA("pool", win_s[:, :, a:b], win_d[:, :, a:b], wr=[winB[i]])
            checkpoint(13)
            checkpoint(14)
            checkpoint(15)
            checkpoint(1)

            def ring_of(arena, n, nbytes, dt, name):
                return Ring([arena.alloc(nbytes, dt) for _ in range(n)], name)

            qkvT_all = AA.alloc(12 * SEQ * 2, BF16).rearrange("p (c n) -> p c n", c=12)
            qkvB_all = [[Buf("qkv%d_%d" % (i, c)) for c in range(12)] for i in range(NS)]
            gates_all = AA.alloc(NT * 32 * 4, F32).rearrange("p (t c) -> p t c", t=NT)
            gateB = [Buf("gate%d" % t) for t in range(NT)]
            halo = AA.alloc(12 * 3 * 4, F32)[:, 0:36].rearrange("p (c j) -> p c j", c=12)
            haloB = [Buf("halo%d" % c) for c in range(12)]
            halo2 = AA.alloc(64, F32)[:, 0:8].rearrange("p (c j) -> p c j", c=4)
            halo2B = [Buf("halo2_%d" % c) for c in range(4)]
            XIN = ring_of(AA2, 1, 4096, F32, "xin")
            ABy = Arena(arena_t, BS_BASE + 8192, 8192)
            XN = Ring([AA.alloc(2048, BF16), ABy.alloc(2048, BF16)], "xn")
            hTs = [AA.alloc(8 * STOK * 2, BF16).rearrange("p (k n) -> p k n", k=8),
                   ABy.alloc(8 * STOK * 2, BF16).rearrange("p (k n) -> p k n", k=8)]
            hTBs = [[Buf("hT%d_%d" % (i, t)) for t in range(ST)] for i in range(2)]
            hTB = hTBs[0] + hTBs[1]
            SCOL = ring_of(AA, 8, 64, F32, "scol")
            ABx = Arena(arena_t, BS_BASE + 20 * 1024, BS_SZ - 21 * 1024)
            PRE = Ring([AA.alloc((STOK + 3) * 4, F32) for _ in range(2)] + [ABx.alloc((STOK + 3) * 4, F32) for _ in range(3)], "pre")
            ACC = Ring([AA.alloc(STOK * 4, F32) for _ in range(2)] + [ABx.alloc(STOK * 4, F32) for _ in range(3)], "acc")
            FZ2 = ABx.alloc(2048, F32)
            PBA = Ring([pb[:, :] for pb in pbanks], "pba", bank=True)
            SCS = ring_of(AA, 1, STOK * 4, F32, "scs")
            PRE2 = ring_of(AA, 1, (STOK + 2) * 4, F32, "pre2")
            SQ = Ring([AA2.alloc(STOK * 2, BF16), ABx.alloc(STOK * 2, BF16)], "sq")
            RSTD = ring_of(AA2, 1, STOK * 4, F32, "rstd")
            FZ = Ring([AA2.alloc(2048, F32), FZ2], "fz")
            print("phase A1 arena used", AA.cur, "of", AA.size, "| tail", AA2.cur, "of", AA2.size)
            allA_rings = [XN, SCOL, PRE, ACC, SCS, PRE2]
            allA2_rings = [XIN, SQ, RSTD, FZ]

            for c in range(12):
                MEMSET("pool", halo[:, c, :], 0.0, wr=[haloB[c]])
            for c in range(4):
                MEMSET("pool", halo2[:, c, :], 0.0, wr=[halo2B[c]])
            checkpoint(16)
            ocatT = R3.rearrange("p (k n) -> p k n", k=8)
            ocatB = [Buf("ocat%d" % t) for t in range(NT)]
            alias_init(ocatB, R3_stage_bufs)

            def v4(ap):
                return ap.rearrange("p (h n) -> p h n", h=4)

            def bc_inner(col4):
                return col4.unsqueeze(2).to_broadcast([128, 4, 128])

            def bc_mid(m):
                return m.unsqueeze(1).to_broadcast([128, 4, 128])

            def rsqrt_col(dst, src, scale, eps, rd, wr):
                ACT(dst, src, AF.Ln, bias=eps, scale=scale, rd=rd, wr=wr)
                ACT(dst, dst, AF.Exp, scale=-0.5, rd=wr, wr=wr)

            def run_window(gens, W):
                gens = list(gens)
                active = []
                i = 0
                while active or i < len(gens):
                    if i < len(gens) and len(active) < W:
                        active.append(gens[i])
                        i += 1
                    for g in list(active):
                        try:
                            next(g)
                        except StopIteration:
                            active.remove(g)
                    yield


            tiles = {}

            def gen_ln_tile(s, t):
                tg = s * ST + t
                hTc, hTBc = hTs[s % 2], hTBs[s % 2]
                xin, xinB = XIN.take()
                DMA("sp", xin, x_d[tg * 128:(tg + 1) * 128, :], wr=[xinB])
                sc, scB = SCOL.take()
                xn, xnB = XN.take()
                ACT(xn, xin, AF.Square, accum=sc[:, 0:1], rd=[xinB], wr=[xnB, scB])
                rsqrt_col(sc[:, 1:2], sc[:, 0:1], 1.0 / D, EPS, [scB], [scB])
                ACT(xn, xin, AF.Identity, scale=sc[:, 1:2], rd=[xinB, scB], wr=[xnB])
                yield
                pt, ptB = PT.take()
                for kc in range(8):
                    TR(pt[:, kc * 128:(kc + 1) * 128], xn[:, kc * 128:(kc + 1) * 128], ident, rd=[xnB, identB], wr=[ptB])
                for kc in range(8):
                    dst = hTc[:, kc, t * 128:(t + 1) * 128]
                    src = pt[:, kc * 128:(kc + 1) * 128]
                    if kc < 4:
                        ACT(dst, src, AF.Identity, bias=modc[:, 1, kc:kc + 1], scale=modc[:, 0, kc:kc + 1],
                            rd=[ptB, modcB], wr=[hTBc[t]])
                    else:
                        TS("dve", dst, src, modc[:, 0, kc:kc + 1], ALU.mult, modc[:, 1, kc:kc + 1], ALU.add,
                           rd=[ptB, modcB], wr=[hTBc[t]])
                yield
                if s == 0 and t == ST - 1:
                    dump("hT0", hTc.rearrange("p k n -> p (k n)"), hTBc, BF16)

            def proj_items(s):
                qk, qkB = qkvT_all[:, :, s * STOK:(s + 1) * STOK], qkvB_all[s]
                hT, hB = hTs[s % 2], hTBs[s % 2]

                def proj_chunk(c0):
                    ps, psB = PBA.take()
                    for kc in range(8):
                        MM(ps[:, 0:STOK], win_s[:, kc, c0:c0 + 128], hT[:, kc, :], start=(kc == 0), stop=(kc == 7),
                           rd=[win_buf(c0)] + hB, wr=[psB])
                    return ps, psB

                def gen_chunk(ch):
                    ps, psB = proj_chunk(ch * 128)
                    pre, preB = PRE.take()
                    CP("act", pre[:, 0:3], halo[:, ch, :], rd=[haloB[ch]], wr=[preB])
                    CP("act", pre[:, 3:3 + STOK], ps[:, 0:STOK], rd=[psB], wr=[preB])
                    CP("act", halo[:, ch, :], pre[:, STOK:STOK + 3], rd=[preB], wr=[haloB[ch]])
                    yield
                    acc, accB = ACC.take()
                    ACT(acc, pre[:, 0:STOK], AF.Identity, scale=cwt[:, ch, 0:1], rd=[preB, cwB], wr=[accB])
                    for j in (1, 2, 3):
                        STT(acc, pre[:, j:j + STOK], cwt[:, ch, j:j + 1], acc, ALU.mult, ALU.add, rd=[preB, cwB, accB], wr=[accB])
                    yield
                    ACT(qk[:, ch, :], acc, AF.Silu, rd=[accB], wr=[qkB[ch]])
                    yield
                    if s == 0 and ch == 11:
                        dump("qkv_silu0", qk, qkB, BF16)

                def gen_l2(ch):
                    sqs = []
                    for c in (ch, ch + 1):
                        sq, sqB = SQ.take()
                        ACT(sq, qk[:, c, :], AF.Square, rd=[qkB[c]], wr=[sqB])
                        sqs.append((sq, sqB))
                    yield
                    ps, psB = PBA.take()
                    for i2, (sq, sqB) in enumerate(sqs):
                        MM(ps[:, i2 * STOK:(i2 + 1) * STOK], onesb, sq, rd=[onesbB, sqB], wr=[psB])
                    rs, rsB = FZ.take()
                    ACT(rs, ps, AF.Ln, bias=EPS, rd=[psB], wr=[rsB])
                    ACT(rs, rs, AF.Exp, scale=-0.5, rd=[rsB], wr=[rsB])
                    yield
                    qv = qk[:, ch:ch + 2, :]
                    rv = rs.rearrange("p (c n) -> p c n", c=2)
                    if ch < 4:
                        STT(qv, qv, DH ** -0.5, rv, ALU.mult, ALU.mult, rd=[qkB[ch], qkB[ch + 1], rsB], wr=[qkB[ch], qkB[ch + 1]])
                    else:
                        TT("dve", qv, qv, rv, ALU.mult, rd=[qkB[ch], qkB[ch + 1], rsB], wr=[qkB[ch], qkB[ch + 1]])
                    yield
                    if s == 0 and ch == 6:
                        dump("qkv_n0", qk, qkB, BF16)

                def gen_z(t):
                    tg = s * ST + t
                    T = tiles.setdefault(tg, {})
                    tcols = slice(t * 128, (t + 1) * 128)
                    ps, psB = PBA.take()
                    for kc in range(8):
                        MM(ps, hT[:, kc, tcols], win_s[:, kc, 1536:2048], start=(kc == 0), stop=(kc == 7),
                           rd=[hB[t], win_buf(1536)], wr=[psB])
                    zg, zgB = ocatT[:, 0:4, tg * 128:(tg + 1) * 128], ocatB[tg]
                    ACT(zg, v4(ps), AF.Silu, rd=[psB], wr=[zgB])
                    T["zgw"] = (zg, zgB)
                    yield

                def gen_zg(t):
                    tg = s * ST + t
                    T = tiles.setdefault(tg, {})
                    tcols = slice(t * 128, (t + 1) * 128)
                    zg, zgB = T["zgw"]
                    ps, psB = PBA.take()
                    for kc in range(8):
                        MM(ps[:, 0:8], hT[:, kc, tcols], win_s[:, kc, 2048:2056], start=(kc == 0), stop=(kc == 7),
                           rd=[hB[t], win_buf(2048)], wr=[psB])
                    gt, gtB = gates_all[:, tg, :], gateB[tg]
                    wk, wkB = SCOL.take()
                    CP("act", wk[:, 0:8], ps[:, 0:8], rd=[psB], wr=[wkB])
                    T["gate"] = (gt, gtB)
                    T["qk"] = (qk, qkB, tcols)
                    if tg == 0:
                        dump("gate0", wk, [wkB])
                        dump("zgw0", zg, [zgB], BF16)
                    yield
                    xg, ax = wk[:, 8:12], wk[:, 12:16]
                    TT("dve", xg, wk[:, 0:4], dtb, ALU.add, rd=[wkB, smallB], wr=[wkB])
                    STT(ax, xg, -1.0, xg, ALU.mult, ALU.max, rd=[wkB], wr=[wkB])
                    yield
                    ACT(ax, ax, AF.Exp, scale=-1.0, rd=[wkB], wr=[wkB])
                    ACT(ax, ax, AF.Ln, bias=1.0, rd=[wkB], wr=[wkB])
                    ACT(wk[:, 4:8], wk[:, 4:8], AF.Exp, scale=-1.0, rd=[wkB], wr=[wkB])
                    ACT(wk[:, 4:8], wk[:, 4:8], AF.Ln, bias=1.0, rd=[wkB], wr=[wkB])
                    ACT(gt[:, 4:8], wk[:, 4:8], AF.Exp, scale=-1.0, rd=[wkB], wr=[gtB])
                    yield
                    STT(xg, xg, 0.0, ax, ALU.max, ALU.add, rd=[wkB], wr=[wkB])
                    TT("dve", gt[:, 0:4], xg, negA, ALU.mult, rd=[wkB, smallB], wr=[gtB])
                    yield
                    ps, psB = PBA.take()
                    MM(ps[:, 0:4], tri, gt[:, 0:4], rd=[triB, gtB], wr=[psB])
                    MM(ps[:, 4:8], onesf, gt[:, 0:4], rd=[onesfB, gtB], wr=[psB])
                    CP("act", gt[:, 8:16], ps[:, 0:8], rd=[psB], wr=[gtB])
                    yield
                    TS("dve", gt[:, 16:20], gt[:, 8:12], -1.0, ALU.mult, rd=[gtB], wr=[gtB])
                    TT("dve", wk[:, 0:4], gt[:, 12:16], gt[:, 8:12], ALU.subtract, rd=[gtB], wr=[wkB])
                    yield
                    ACT(gt[:, 20:28], gt[:, 8:16], AF.Exp, rd=[gtB], wr=[gtB])
                    ACT(gt[:, 28:32], wk[:, 0:4], AF.Exp, rd=[wkB], wr=[gtB])
                    yield

                def gen_sc(i):
                    psC, psCB = proj_chunk(2568 + i * 128)
                    scs, scsB = SCS.take()
                    CP("act", scs, psC[:, 0:STOK], rd=[psCB], wr=[scsB])
                    yield
                    psH, psHB = proj_chunk(3080 + i * 128)
                    pre2, pre2B = PRE2.take()
                    CP("act", pre2[:, 0:2], halo2[:, i, :], rd=[halo2B[i]], wr=[pre2B])
                    TT("dve", pre2[:, 2:2 + STOK], psH[:, 0:STOK], scs, ALU.mult, rd=[psHB, scsB], wr=[pre2B])
                    CP("act", halo2[:, i, :], pre2[:, STOK:STOK + 2], rd=[pre2B], wr=[halo2B[i]])
                    yield
                    acc, accB = ACC.take()
                    ACT(acc, pre2[:, 0:STOK], AF.Identity, scale=scwt[:, i, 0:1], rd=[pre2B, scwB], wr=[accB])
                    for j in (1, 2):
                        STT(acc, pre2[:, j:j + STOK], scwt[:, i, j:j + 1], acc, ALU.mult, ALU.add, rd=[pre2B, scwB, accB], wr=[accB])
                    yield
                    psBb, psBB = proj_chunk(2056 + i * 128)
                    tg0 = s * ST
                    TT("dve", ocatT[:, 4 + i, tg0 * 128:(tg0 + ST) * 128], psBb[:, 0:STOK], acc, ALU.mult,
                       rd=[psBB, accB], wr=[ocatB[tg0 + t2] for t2 in range(ST)])
                    yield

                items = [gen_chunk(ch) for ch in range(12)] + [gen_z(t) for t in range(ST)]
                L2_CH = (0, 2, 4, 6)
                if s + 1 < NS:
                    items += [gen_ln_tile(s + 1, t) for t in range(ST)]
                items += [gen_l2(ch) for ch in L2_CH] + [gen_zg(t) for t in range(ST)] + [gen_sc(i) for i in range(4)]
                return items

            def gen_dn(tg, Bs):
                T = tiles[tg]
                PDr = Bs["PD"]

                class _PDF:
                    @staticmethod
                    def take():
                        ap, bf_ = PDr.take()
                        return ap.bitcast(F32), bf_
                PD = _PDF
                PT = PDr
                FD, UU, GREP, BT, SCOL = Bs["FD"], Bs["UU"], Bs["GREP"], Bs["BT"], SCOL2
                qk, qkB, tcols = T["qk"]
                gt, gtB = T["gate"]
                zg, zgB = T["zgw"]
                qTB = [qkB[h] for h in range(4)]
                kTB = [qkB[4 + h] for h in range(4)]
                g4, beta = gt[:, 0:4], gt[:, 4:8]
                gcB, ecB = gtB, gtB
                gc_map = {(0, 4): gt[:, 8:12], (4, 8): gt[:, 12:16], (8, 12): gt[:, 16:20]}
                ec_map = {(0, 4): gt[:, 20:24], (4, 8): gt[:, 24:28], (8, 12): gt[:, 28:32]}
                grep, grepB = GREP.take()
                TT("dve", v4(grep), bc_mid(onesf), bc_inner(g4), ALU.mult, rd=[onesfB, gtB], wr=[grepB])
                yield
                psg, psgB = PD.take()
                for h in range(NH):
                    MM(psg[:, h * 128:(h + 1) * 128], grep[:, h * 128:(h + 1) * 128], tri, rd=[grepB, triB], wr=[psgB])
                yield
                pt, ptB = PT.take()
                for i2, base in enumerate((4, 8)):
                    for h in range(NH):
                        TR(pt[:, (i2 * 4 + h) * 128:(i2 * 4 + h + 1) * 128], qk[:, base + h, tcols], ident,
                           rd=[qkB[base + h], identB], wr=[ptB])
                ke, keB = Bs["ke"].take()
                kdec, kdecB = Bs["kdec"].take()
                vtok, vtokB = Bs["VTOK"].take()
                CP("act", kdec, pt[:, 0:512], rd=[ptB], wr=[kdecB])
                CP("dve", vtok, pt[:, 512:1024], rd=[ptB], wr=[vtokB])
                yield
                TT("dve", v4(ke), v4(kdec), bc_inner(ec_map[(0, 4)]), ALU.mult, rd=[kdecB, ecB], wr=[keB])
                TT("dve", v4(kdec), v4(kdec), bc_inner(ec_map[(8, 12)]), ALU.mult, rd=[kdecB, ecB], wr=[kdecB])
                yield
                a1, a1B = FD.take()
                TT("dve", v4(a1), v4(psg), bc_mid(pms), ALU.add, rd=[psgB, pmsB], wr=[a1B])
                a2, a2B = FD.take()
                TT("dve", v4(a2), v4(psg), bc_mid(pmt), ALU.subtract, rd=[psgB, pmtB], wr=[a2B])
                yield
                egr, egrB = BT.take()
                ACT(egr, psg, AF.Exp, rd=[psgB], wr=[egrB])
                TT("dve", v4(a1), v4(a1), bc_inner(gc_map[(0, 4)]), ALU.subtract, rd=[a1B, gcB], wr=[a1B])
                TT("dve", v4(a2), v4(a2), bc_inner(gc_map[(0, 4)]), ALU.subtract, rd=[a2B, gcB], wr=[a2B])
                yield
                Dm, DmB = BT.take()
                ACT(Dm, a1, AF.Exp, scale=-1.0, rd=[a1B], wr=[DmB])
                DT, DTB = BT.take()
                ACT(DT, a2, AF.Exp, rd=[a2B], wr=[DTB])
                psk, pskB = PD.take()
                for h in range(NH):
                    MM(psk[:, h * 128:(h + 1) * 128], qk[:, 4 + h, tcols], qk[:, 4 + h, tcols], rd=[kTB[h]], wr=[pskB])
                yield
                Am, AmB = Bs["Am"].take()
                TT("dve", Am, psk, Dm, ALU.mult, rd=[pskB, DmB], wr=[AmB])
                psq, psqB = PD.take()
                for h in range(NH):
                    MM(psq[:, h * 128:(h + 1) * 128], qk[:, 4 + h, tcols], qk[:, h, tcols], rd=[kTB[h], qTB[h]], wr=[psqB])
                yield
                attT, attTB = Bs["attT"].take()
                TT("dve", attT, psq, DT, ALU.mult, rd=[psqB, DTB], wr=[attTB])
                qeT, qeTB = Bs["qeT"].take()
                TT("dve", v4(qeT), qk[:, 0:4, tcols], v4(egr), ALU.mult, rd=qTB + [egrB], wr=[qeTB])
                yield
                D0, D0B = Bs["Y"].take()
                TT("dve", v4(D0), bc_mid(ident), bc_inner(beta), ALU.mult, rd=[identB, gtB], wr=[D0B])
                yield
                X, XB = D0, D0B
                W, WB = D0, D0B
                Xn, XnB = Bs["X"].take()
                Wn, WnB = Bs["W"].take()
                for k in range(7):
                    psy, psyB = PD.take()
                    for h in range(NH):
                        hs = slice(h * 128, (h + 1) * 128)
                        MM(psy[:, hs], Am[:, hs], W[:, hs], rd=[AmB, WB], wr=[psyB])
                    yield
                    Y, YB = Bs["Bk"].take()
                    STT(v4(Y), v4(psy), -1.0, bc_mid(lvlT[:, k, :]), ALU.mult, ALU.mult, rd=[psyB, lvlTB], wr=[YB])
                    yield
                    psz, pszB = PD.take()
                    for h in range(NH):
                        hs = slice(h * 128, (h + 1) * 128)
                        MM(psz[:, hs], X[:, hs], Y[:, hs], start=True, stop=False, rd=[XB, YB], wr=[pszB])
                        MM(psz[:, hs], ident, W[:, hs], start=False, stop=True, rd=[identB, WB], wr=[pszB])
                    if k < 6:
                        pszt, psztB = PD.take()
                        for h in range(NH):
                            hs = slice(h * 128, (h + 1) * 128)
                            MM(pszt[:, hs], Y[:, hs], X[:, hs], start=True, stop=False, rd=[XB, YB], wr=[psztB])
                            MM(pszt[:, hs], ident, X[:, hs], start=False, stop=True, rd=[identB, XB], wr=[psztB])
                    yield
                    CP("act", Wn, psz, rd=[pszB], wr=[WnB])
                    W, WB = Wn, WnB
                    if k < 6:
                        CP("act", Xn, pszt, rd=[psztB], wr=[XnB])
                        X, XB = Xn, XnB
                    yield
                if tg == 0:
                    dump("W0", W, [WB], BF16)
                    dump("A0", Am, [AmB], BF16)
                psu, psuB = PD.take()
                for h in range(NH):
                    hs = slice(h * 128, (h + 1) * 128)
                    MM(psu[:, hs], W[:, hs], vtok[:, hs], rd=[WB, vtokB], wr=[psuB])
                psw, pswB = PD.take()
                for h in range(NH):
                    hs = slice(h * 128, (h + 1) * 128)
                    MM(psw[:, hs], ke[:, hs], W[:, hs], rd=[keB, WB], wr=[pswB])
                yield
                u, uB = UU.take()
                CP("act", u, psu, rd=[psuB], wr=[uB])
                wT, wTB = Bs["wT"].take()
                CP("act", wT, psw, rd=[pswB], wr=[wTB])
                yield
                ps1, ps1B = PD.take()
                for h in range(NH):
                    hs = slice(h * 128, (h + 1) * 128)
                    MM(ps1[:, hs], wT[:, hs], Sbf[:, hs], rd=[wTB, SbfB], wr=[ps1B])
                vn, vnB = Bs["vn"].take()
                TT("dve", vn, u, ps1, ALU.subtract, rd=[uB, ps1B], wr=[vnB])
                pso, psoB = PD.take()
                for h in range(NH):
                    hs = slice(h * 128, (h + 1) * 128)
                    MM(pso[:, hs], qeT[:, hs], Sbf[:, hs], start=True, stop=False, rd=[qeTB, SbfB], wr=[psoB])
                    MM(pso[:, hs], attT[:, hs], vn[:, hs], start=False, stop=True, rd=[attTB, vnB], wr=[psoB])
                ps3, ps3B = PD.take()
                for h in range(NH):
                    hs = slice(h * 128, (h + 1) * 128)
                    MM(ps3[:, hs], kdec[:, hs], vn[:, hs], rd=[kdecB, vnB], wr=[ps3B])
                TT("dve", v4(Sst), v4(Sst), bc_inner(ec_map[(4, 8)]), ALU.mult, rd=[SstB, ecB], wr=[SstB])
                TT("dve", Sst, Sst, ps3, ALU.add, rd=[SstB, ps3B], wr=[SstB])
                CP("act", Sbf, Sst, rd=[SstB], wr=[SbfB])
                oc, ocB = SCOL.take()
                on, onB = Bs["odn"].take()
                for h in range(NH):
                    ACT(on[:, h * 128:(h + 1) * 128], pso[:, h * 128:(h + 1) * 128], AF.Square, accum=oc[:, h:h + 1],
                        rd=[psoB], wr=[onB, ocB])
                yield
                rsqrt_col(oc[:, 4:8], oc[:, 0:4], 1.0 / DH, EPS, [ocB], [ocB])
                TT("dve", v4(on), v4(pso), bc_inner(oc[:, 4:8]), ALU.mult, rd=[psoB, ocB], wr=[onB])
                yield
                TT("dve", v4(on), v4(on), zg, ALU.mult, rd=[onB, zgB], wr=[onB])
                yield
                pt, ptB = PT.take()
                for h in range(NH):
                    TR(pt[:, h * 128:(h + 1) * 128], on[:, h * 128:(h + 1) * 128], ident, rd=[onB, identB], wr=[ptB])
                CP("act", ocatT[:, 0:4, tg * 128:(tg + 1) * 128], v4(pt[:, 0:512]), rd=[ptB], wr=[ocatB[tg]])
                if tg == 0:
                    dump("odn0", on, [onB], BF16)
                    dump("S0", Sst, [SstB])
                yield

            step_ctr = [0]

            def run_interleaved(gens):
                gens = [g for g in gens if g is not None]
                while gens:
                    for g in list(gens):
                        try:
                            next(g)
                        except StopIteration:
                            gens.remove(g)
                        step_ctr[0] += 1
                        checkpoint(1000 + step_ctr[0])

            def chain(*gs):
                for g in gs:
                    yield from g

            alias_init(PBA.bufs(), PB.bufs() + PD.bufs())
            def gen_ada(j):
                ada_dma(j)
                for _ in range(10):
                    yield
                ps, psB = PBA.take()
                ada_slice(j, ps, psB, dma=False)
                yield

            a1_items = [gen_ln_tile(0, t) for t in range(ST)]
            for s in range(NS):
                a1_items += proj_items(s)
                a1_items.append(gen_ada(4 + s))
            for _ in run_window(a1_items, 9):
                step_ctr[0] += 1
                checkpoint(2000 + step_ctr[0])

            dump("modc", modc.rearrange("p v k -> p (v k)"), [modcB, modcBf])
            dump("Gm", Gm, [GmB])
            dump("Gf", Gf, [GfB])
            AR1 = Arena(arena_t, P_SZ, R1_SZ)
            ABd = Arena(arena_t, BS_BASE, BS_SZ - 1024)
            Sst = AR1.alloc(2048, F32)
            SstB = Buf("S")
            Sbf = AR1.alloc(1024, BF16)
            SbfB = Buf("Sbf")
            SCOL2 = ring_of(AR1, 16, 64, F32, "scol2")

            def make_dn_set(ar, pd, tag):
                Bs = dict(
                    VTOK=ring_of(ar, 1, 1024, BF16, "vtok" + tag),
                    FD=ring_of(ar, 2, 2048, F32, "fd" + tag), UU=ring_of(ar, 1, 2048, F32, "uu" + tag),
                    GREP=ring_of(ar, 1, 2048, F32, "grep" + tag), BT=ring_of(ar, 3, 1024, BF16, "bt" + tag),
                    PD=Ring([pb for pb in pd], "pbd" + tag, bank=True))
                for nm, n in (("Am", 1), ("attT", 1), ("qeT", 1), ("ke", 1), ("kdec", 1), ("X", 1), ("W", 1), ("Bk", 2),
                              ("Y", 1), ("wT", 1), ("vn", 1), ("odn", 1)):
                    Bs[nm] = ring_of(ar, n, 1024, BF16, nm + tag)
                return Bs

            class MultiArena:
                def __init__(self, arenas):
                    self.arenas = arenas

                def alloc(self, nbytes, dt=BF16):
                    need = (nbytes + 63) // 64 * 64
                    for a_ in self.arenas:
                        if a_.cur + need <= a_.size:
                            return a_.alloc(nbytes, dt)
                    raise AssertionError("multi-arena overflow")

            AA3 = Arena(arena_t, A_BASE + 12 * SEQ * 2 + NT * 32 * 4, AA.cur - (12 * SEQ * 2 + NT * 32 * 4))
            DNS = [make_dn_set(AR1, banks_bf[0:2], "a"), make_dn_set(AR1, banks_bf[2:4], "b"), make_dn_set(ABd, banks_bf[4:6], "c")]
            DNS.append(make_dn_set(MultiArena([AA3, AR1, ABd]), banks_bf[6:8], "d"))
            dn_rings = lambda Bs: [v for k, v in Bs.items() if k != "PD"]
            print("phase A2: R1 used", AR1.cur, "of", AR1.size, "| set c", ABd.cur, "of", ABd.size)
            r1_bufs_now = lambda: ([SstB, SbfB] + SCOL2.bufs() + sum([r.bufs() for r in dn_rings(DNS[0]) + dn_rings(DNS[1])], [])
                                   + winB + sum([r.bufs() for r in allA2_rings], []))
            r1_bufs = [SstB, SbfB] + SCOL2.bufs() + sum([r.bufs() for r in dn_rings(DNS[0]) + dn_rings(DNS[1])], [])
            alias_init(r1_bufs, winB + sum([r.bufs() for r in allA2_rings], []))
            bsx_bufs = PRE.bufs() + ACC.bufs() + FZ.bufs() + SQ.bufs() + XN.bufs() + hTB
            alias_init(sum([r.bufs() for r in dn_rings(DNS[2])], []), [wst2B] + bstB + bsx_bufs)
            for i in range(3):
                alias_init(DNS[i]["PD"].bufs(), PBA.bufs())
            alias_init(DNS[3]["PD"].bufs(), PT.bufs())
            old_everything = (winB + sum([r.bufs() for r in allA2_rings + allA_rings], []) + haloB + halo2B + hTB + bsx_bufs
                              + [wst2B] + bstB)
            set_d_bufs = lambda: sum([r.bufs() for r in dn_rings(DNS[3])], [])
            alias_init(set_d_bufs(), old_everything)
            MEMSET("pool", Sst, 0.0, wr=[SstB])
            MEMSET("pool", Sbf, 0.0, wr=[SbfB])
            active = []
            nxt_tile = 0
            while active or nxt_tile < NT:
                if nxt_tile < NT and len(active) < 4:
                    active.append(gen_dn(nxt_tile, DNS[nxt_tile % 4]))
                    nxt_tile += 1
                for g in list(active):
                    try:
                        next(g)
                    except StopIteration:
                        active.remove(g)
                step_ctr[0] += 1
                checkpoint(1000 + step_ctr[0])
            dump("ocat", ocatT[:, :, 0:256], ocatB[0:2], BF16)
            checkpoint(4)

            alias_init(PD.bufs(), DNS[1]["PD"].bufs() + DNS[2]["PD"].bufs())
            alias_init(PB.bufs(), DNS[0]["PD"].bufs())
            alias_init(PT.bufs(), DNS[3]["PD"].bufs())
            alias_init(woutB, sum([r.bufs() for r in dn_rings(DNS[2])], []) + [wst2B] + bstB + XN.bufs() + hTB)
            for i in range(2):
                DMA("pool", wout_s[:, :, i * 512:(i + 1) * 512], wout_d[:, :, i * 512:(i + 1) * 512], wr=[woutB[i]])
            for i in range(2):
                TS("dve", wout_s[:, 0:4, i * 512:(i + 1) * 512], wout_s[:, 0:4, i * 512:(i + 1) * 512], nwbc[:, 0:1], ALU.mult,
                   rd=[nwbcB], wr=[woutB[i]])
            ff1_s = RA.rearrange("p (k n) -> p k n", k=8)
            ff1B = [Buf("ff1_%d" % i) for i in range(8)]
            allA = (haloB + halo2B + hTB + gateB + [b for q in qkvB_all for b in q] + sum([r.bufs() for r in allA_rings], [])
                    + set_d_bufs())
            alias_init(ff1B, allA)
            for i in range(8):
                DMA("pool", ff1_s[:, :, i * 512:(i + 1) * 512], wff1_d[:, :, i * 512:(i + 1) * 512], wr=[ff1B[i]])
            ff2_s = R1.rearrange("p (k n) -> p k n", k=32)
            ff2B = [Buf("ff2_%d" % i) for i in range(8)]
            alias_init(ff2B, r1_bufs_now() + set_d_bufs())
            for i in range(8):
                DMA("pool", ff2_s[:, i * 4:(i + 1) * 4, :], wff2_d[:, i * 4:(i + 1) * 4, :], wr=[ff2B[i]])

            ABlo = Arena(arena_t, BS_BASE, 16 * 1024)
            ABhi = Arena(arena_t, BS_BASE + 16 * 1024, BS_SZ - 17 * 1024)
            ABj = Arena(arena_t, BS_BASE + BS_SZ - 1024, 1024)
            TB = 2
            NG = NT // TB
            NHS = 8
            xb_hi = [ABhi.alloc(4096, F32) for _ in range(2)]
            xb_lo = [ABlo.alloc(4096, F32) for _ in range(2)]
            XB_ = Ring(xb_hi + xb_lo, "xb")
            XN2 = Ring([ABhi.alloc(2048, BF16)], "xn2")
            h2Ts = [ABhi.alloc(8 * TB * 128 * 2, BF16).rearrange("p (k n) -> p k n", k=8),
                    ABlo.alloc(8 * TB * 128 * 2, BF16).rearrange("p (k n) -> p k n", k=8)]
            h2Bs = [[[Buf("h2_%d_%d_%d" % (i, t, k)) for k in range(8)] for t in range(TB)] for i in range(2)]
            hidT = ABlo.alloc(NHS * TB * 128 * 2, BF16).rearrange("p (k n) -> p k n", k=NHS)
            hidB = [[Buf("hid_%d_%d" % (t, k)) for k in range(NHS)] for t in range(TB)]
            BCOL = Ring([ABhi.alloc(64, F32) for _ in range(4)], "bcol")
            RTMP = Ring([ABj.alloc(128 * 4, F32) for _ in range(2)], "rtmp")
            print("phase B arenas used: lo", ABlo.cur, "of", ABlo.size, "| hi", ABhi.cur, "of", ABhi.size)
            hi_bufs = XB_.bufs()[0:2] + XN2.bufs() + [b for r in h2Bs[0] for b in r] + BCOL.bufs()
            lo_bufs = XB_.bufs()[2:4] + [b for r in h2Bs[1] for b in r] + [b for r in hidB for b in r] + RTMP.bufs()
            dead_after_dn = sum([r.bufs() for r in dn_rings(DNS[2])], []) + set_d_bufs() + bstB + [wst2B] + bsx_bufs
            alias_init(hi_bufs, dead_after_dn)

            def y_view(tg):
                return ocatT[:, :, tg * 128:(tg + 1) * 128]

            xbs_of = {}

            def gen_lnb(gI):
                h2T, h2B = h2Ts[gI % 2], h2Bs[gI % 2]
                for t in range(TB):
                    tg = gI * TB + t
                    xb, xbB = XB_.take()
                    xbs_of[(gI, t)] = (xb, xbB)
                    DMA("sp", xb, x_d[tg * 128:(tg + 1) * 128, :], wr=[xbB])
                    bc, bcB = BCOL.take()
                    yv = y_view(tg)
                    TT("dve", bc[:, 0:1], ysq[:, tg * 2:tg * 2 + 1], ysq[:, tg * 2 + 1:tg * 2 + 2], ALU.add, rd=[ysqB], wr=[bcB])
                    yield
                    rsqrt_col(bc[:, 1:2], bc[:, 0:1], 1.0 / D, EPS, [bcB], [bcB])
                    yield
                    xn2, xn2B = XN2.take()
                    tmpf = xn2.bitcast(F32)
                    for nh in range(2):
                        STT(v4(tmpf), yv[:, nh * 4:(nh + 1) * 4, :], bc[:, 1:2], v4(Gm[:, nh * 512:(nh + 1) * 512]), ALU.mult, ALU.mult,
                            rd=[ocatB[tg], bcB, GmB], wr=[xn2B])
                        TT("dve", xb[:, nh * 512:(nh + 1) * 512], xb[:, nh * 512:(nh + 1) * 512], tmpf, ALU.add, rd=[xbB, xn2B], wr=[xbB])
                        yield
                    ACT(xn2, xb, AF.Square, accum=bc[:, 2:3], rd=[xbB], wr=[xn2B, bcB])
                    yield
                    rsqrt_col(bc[:, 3:4], bc[:, 2:3], 1.0 / D, EPS, [bcB], [bcB])
                    yield
                    ACT(xn2, xb, AF.Identity, scale=bc[:, 3:4], rd=[xbB, bcB], wr=[xn2B])
                    if tg == 0:
                        dump("x1_0", xb, [xbB])
                    yield
                    pt, ptB = PT.take()
                    for kc in range(8):
                        TR(pt[:, kc * 128:(kc + 1) * 128], xn2[:, kc * 128:(kc + 1) * 128], ident, rd=[xn2B, identB], wr=[ptB])
                    for kc in range(8):
                        dst = h2T[:, kc, t * 128:(t + 1) * 128]
                        src = pt[:, kc * 128:(kc + 1) * 128]
                        if kc < 4:
                            ACT(dst, src, AF.Identity, bias=modc[:, 3, kc:kc + 1], scale=modc[:, 2, kc:kc + 1],
                                rd=[ptB, modcBf], wr=[h2B[t][kc]])
                        else:
                            TS("dve", dst, src, modc[:, 2, kc:kc + 1], ALU.mult, modc[:, 3, kc:kc + 1], ALU.add,
                               rd=[ptB, modcBf], wr=[h2B[t][kc]])
                    yield

            pss_of = {}

            def gen_ffn(gI):
                h2T, h2B = h2Ts[gI % 2], h2Bs[gI % 2]
                for hh in range(32 // NHS):
                    for f in range(NHS):
                        fc = hh * NHS + f
                        ps, psB = PB.take()
                        for kc in range(8):
                            MM(ps[:, 0:TB * 128], ff1_s[:, kc, fc * 128:(fc + 1) * 128], h2T[:, kc, :], start=(kc == 0), stop=(kc == 7),
                               rd=[ff1B[fc // 4]] + [h2B[t][kc] for t in range(TB)], wr=[psB])
                        for t in range(TB):
                            rt, rtB = RTMP.take()
                            ACT(rt, ps[:, t * 128:(t + 1) * 128], AF.Relu, rd=[psB], wr=[rtB])
                            TT("dve", hidT[:, f, t * 128:(t + 1) * 128], rt, rt, ALU.mult, rd=[rtB], wr=[hidB[t][f]])
                        yield
                    if hh == 0:
                        pss_of[gI] = [[PD.take(), PD.take()] for t in range(TB)]
                    pss = pss_of[gI]
                    for t in range(TB):
                        for f in range(NHS):
                            fc = hh * NHS + f
                            for nh in range(2):
                                ps, psB = pss[t][nh]
                                MM(ps, hidT[:, f, t * 128:(t + 1) * 128], ff2_s[:, fc, nh * 512:(nh + 1) * 512], start=(fc == 0), stop=(fc == 31),
                                   rd=[hidB[t][f], ff2B[fc // 4]], wr=[psB])
                            if f % 4 == 3:
                                yield

            def gen_epi(gI):
                pss = pss_of[gI]
                for t in range(TB):
                    tg = gI * TB + t
                    xb, xbB = xbs_of.pop((gI, t))
                    bc, bcB = BCOL.take()
                    jn = h2Ts[gI % 2][:, 0:4, t * 128:(t + 1) * 128]
                    jnB = [h2Bs[gI % 2][t][k] for k in range(4)]
                    for nh in range(2):
                        ps, psB = pss[t][nh]
                        ACT(jn, v4(ps), AF.Square, accum=bc[:, nh:nh + 1], rd=[psB], wr=jnB + [bcB])
                    yield
                    TT("dve", bc[:, 2:3], bc[:, 0:1], bc[:, 1:2], ALU.add, rd=[bcB], wr=[bcB])
                    yield
                    rsqrt_col(bc[:, 3:4], bc[:, 2:3], 1.0 / D, EPS, [bcB], [bcB])
                    yield
                    for nh in range(2):
                        ps, psB = pss[t][nh]
                        STT(ps, ps, bc[:, 3:4], Gf[:, nh * 512:(nh + 1) * 512], ALU.mult, ALU.mult, rd=[psB, bcB, GfB], wr=[psB])
                        yield
                        TT("dve", xb[:, nh * 512:(nh + 1) * 512], xb[:, nh * 512:(nh + 1) * 512], ps, ALU.add, rd=[xbB, psB], wr=[xbB])
                        yield
                    DMA("sp", out_d[tg * 128:(tg + 1) * 128, :], xb, rd=[xbB], wr=[OUTB])
                    yield

            def drain(g):
                for _ in g:
                    pass

            def step(g, n=1):
                for _ in range(n):
                    try:
                        next(g)
                    except StopIteration:
                        return False
                return True

            ln0 = gen_lnb(0)
            ln0_alive = True
            for tg in range(NT):
                tc = slice(tg * 128, (tg + 1) * 128)
                pss = [PD.take(), PD.take()]
                for nh in range(2):
                    ps, psB = pss[nh]
                    for kc in range(8):
                        MM(ps, ocatT[:, kc, tc], wout_s[:, kc, nh * 512:(nh + 1) * 512], start=(kc == 0), stop=(kc == 7),
                           rd=[ocatB[tg], woutB[nh]], wr=[psB])
                for nh in range(2):
                    ps, psB = pss[nh]
                    ACT(junk0, ps, AF.Square, accum=ysq[:, tg * 2 + nh:tg * 2 + nh + 1], rd=[psB], wr=[junk0B, ysqB])
                    CP("dve", ocatT[:, nh * 4:(nh + 1) * 4, tc], v4(ps), rd=[psB], wr=[ocatB[tg]])
                if tg >= TB and ln0_alive:
                    ln0_alive = step(ln0, 2)
            drain(ln0)
            alias_init(lo_bufs, woutB + B0_bufs + dead_after_dn)
            dump("y0", ocatT[:, :, 0:128], [ocatB[0]], BF16)
            checkpoint(5)

            NG = NT // TB
            for gI in range(NG):
                main = gen_ffn(gI)
                side = chain(gen_epi(gI - 1) if gI > 0 else iter(()), gen_lnb(gI + 1) if gI + 1 < NG else iter(()))
                rounds = 0
                main_alive, side_alive = True, True
                while main_alive:
                    main_alive = step(main)
                    if side_alive:
                        side_alive = step(side, 2 if rounds < NHS - 1 else 1)
                    rounds += 1
                drain(side)
            drain(gen_epi(NG - 1))

            S.emit(final_bufs=[OUTB])
    except _Stop:
        pass
    print("ops (total, signalling):", S.stats)
    return nc, dbg_outs


def _consts():
    i = np.arange(128)
    ident = np.eye(128, dtype=np.float32).astype(ml_dtypes.bfloat16)
    tri = (i[:, None] <= i[None, :]).astype(np.float32)
    pm_strict = np.where(i[None, :] < i[:, None], 0.0, BIG).astype(np.float32)
    pm_t = np.where(i[None, :] >= i[:, None], 0.0, BIG).astype(np.float32)
    lv = np.zeros((128, 7, 128), np.float32)
    for k in range(7):
        s = 1 << k
        same = (i[:, None] // (2 * s)) == (i[None, :] // (2 * s))
        up = (i[:, None] % (2 * s)) >= s
        lo = (i[None, :] % (2 * s)) < s
        lv[:, k, :] = (same & up & lo).astype(np.float32).T
    return dict(ident_bf=ident, tri=tri, pm_strict=pm_strict, pm_t=pm_t, lvlmaskT=lv.astype(ml_dtypes.bfloat16))


def make_in_maps(x, c, w_ada, b_ada, pre_mix_norm_w, post_mix_norm_w, w_in, dn_conv_w, dn_a_log, dn_dt_bias,
                 dn_norm_w, sc_conv_w, w_out, pre_ffn_norm_w, post_ffn_norm_w, w_ff1, w_ff2):
    f = lambda a: np.ascontiguousarray(np.asarray(a, dtype=np.float32))
    x, c = f(x), f(c)
    shared = dict(
        w_ada=f(w_ada[0]), b_ada=f(b_ada[0]),
        b_ada_col=f(np.asarray(b_ada[0]).reshape(6, 8, 128).transpose(2, 0, 1)),
        w_pre_col=f(np.asarray(pre_mix_norm_w[0]).reshape(8, 128).T),
        w_preffn_col=f(np.asarray(pre_ffn_norm_w[0]).reshape(8, 128).T),
        w_post=f(post_mix_norm_w[0]), w_postffn=f(post_ffn_norm_w[0]),
        w_in=f(w_in[0]),
        dn_conv_col=f(np.asarray(dn_conv_w[0]).T.reshape(12, 128, 4).transpose(1, 0, 2)),
        dn_a_log=f(dn_a_log[0]), dn_dt_bias=f(dn_dt_bias[0]), dn_norm_w=f(dn_norm_w[0]),
        sc_conv_col=f(np.asarray(sc_conv_w[0]).T.reshape(4, 128, 3).transpose(1, 0, 2)),
        w_out=f(w_out[0]), w_ff1=f(w_ff1[0]), w_ff2=f(w_ff2[0]),
    )
    shared.update(_consts())
    maps = []
    for b in range(NCORES):
        m = dict(shared)
        m["x"] = x[b]
        m["ccol"] = f(c[b].reshape(8, 128).T)
        maps.append(m)
    return maps


def kernel(**inputs):
    nc, _ = build_program(dbg=False)
    in_maps = make_in_maps(**inputs)
    res = run_bass_kernel_spmd(nc, in_maps, core_ids=list(range(NCORES)))
    return np.stack([np.asarray(r["out"], dtype=np.float32) for r in res.results], axis=0)
```

```python
import bisect
import contextlib
import math

import numpy as np
import ml_dtypes
import concourse.bass as bass
import concourse.mybir as mybir
from concourse.bass_utils import run_bass_kernel_spmd

F32 = mybir.dt.float32
BF16 = mybir.dt.bfloat16
ALU = mybir.AluOpType
AF = mybir.ActivationFunctionType

NCORES = 8
SEQ = 2048
D = 1024
DIN = 3592
NH = 4
DH = 128
DFF = 4096
EPS = 1e-6
NT = SEQ // 128
ST = 2
NS = NT // ST
STOK = ST * 128
BIG = 30000.0


class Buf:
    __slots__ = ("name", "w", "r", "bank", "dead")

    def __init__(self, name="", bank=False):
        self.name = name
        self.w = None
        self.r = []
        self.dead = False
        self.bank = bank


def alias_init(newbufs, oldbufs):
    deps = []
    for o in oldbufs:
        if o.w is not None:
            deps.append(o.w)
        deps.extend(o.r)
    best = {}
    for d in deps:
        k = d[:2] if d[0] == "e" else d[:3]
        if k not in best or d[-1] > best[k][-1]:
            best[k] = d
    deps = list(best.values())
    for n in newbufs:
        n.r = list(deps) + n.r


class Sched:
    ENGS = ("pe", "act", "dve", "pool", "sp")
    NDMA = 12

    def __init__(self, nc):
        self.nc = nc
        self.ops = {e: [] for e in self.ENGS}
        self.last_sig = {e: -1 for e in self.ENGS}
        self.seen = {e: {} for e in self.ENGS}
        self.dma_cnt = {}
        self.dma_rr = {e: 0 for e in self.ENGS}
        self.last_compute = {e: -1 for e in self.ENGS}

    def _add_dep(self, eng, rec, dep):
        if dep is None:
            return
        if dep[0] == "e":
            _, e2, o2 = dep
            if e2 == eng and eng == "pe":
                return
            if self.seen[eng].get(("e", e2), -1) >= o2:
                return
            self.seen[eng][("e", e2)] = o2
            if self.last_sig[e2] < o2:
                last = self.ops[e2][self.last_compute[e2]]
                assert last["ord"] >= o2 and last["fn"] is not None
                last["signal"] = True
                self.last_sig[e2] = last["ord"]
            rec["deps"].append(dep)
        else:
            _, q, k, val = dep
            if self.seen[eng].get(("d", q, k), -1) >= val:
                return
            self.seen[eng][("d", q, k)] = val
            rec["deps"].append(dep)

    def _collect(self, eng, rec, reads, writes):
        for b in list(reads) + list(writes):
            assert not b.dead, ("use of a re-taken ring buffer", b.name)
        for b in reads:
            self._add_dep(eng, rec, b.w)
            if b.bank:
                for r in b.r:
                    if r[0] == "e" and r[1] != eng:
                        self._add_dep(eng, rec, r)
        for b in writes:
            self._add_dep(eng, rec, b.w)
            for r in b.r:
                self._add_dep(eng, rec, r)

    @staticmethod
    def _commit(key, reads, writes):
        for b in reads:
            b.r.append(key)
            if len(b.r) > 64:
                b.r = b.r[-64:] if False else b.r
        for b in writes:
            b.w = key
            b.r = []

    def op(self, eng, fn, reads=(), writes=()):
        rec = dict(ord=len(self.ops[eng]), deps=[], fn=fn, signal=False, dma=None)
        self._collect(eng, rec, reads, writes)
        self.ops[eng].append(rec)
        self.last_compute[eng] = rec["ord"]
        self._commit(("e", eng, rec["ord"]), reads, writes)
        return rec

    def dma(self, queue, out, in_, reads=(), writes=()):
        k = self.dma_rr[queue]
        self.dma_rr[queue] = (k + 1) % self.NDMA
        n = self.dma_cnt.get((queue, k), 0)
        rec = dict(ord=len(self.ops[queue]), deps=[], fn=None, signal=False, dma=(k, out, in_))
        if n > 0:
            self._add_dep(queue, rec, ("d", queue, k, 16 * n))
        self._collect(queue, rec, reads, writes)
        self.ops[queue].append(rec)
        self.dma_cnt[(queue, k)] = n + 1
        self._commit(("d", queue, k, 16 * (n + 1)), reads, writes)
        return rec

    def emit(self, final_bufs=()):
        nc = self.nc
        with contextlib.ExitStack() as st:
            esem = {e: st.enter_context(nc.semaphore("s_" + e)) for e in self.ENGS}
            dsem = {}
            for (q, k) in sorted(self.dma_cnt):
                dsem[(q, k)] = st.enter_context(nc.semaphore("d_%s_%d" % (q, k)))
            if final_bufs:
                rec = dict(ord=len(self.ops["sp"]), deps=[], fn=None, signal=False, dma=None)
                for b in final_bufs:
                    self._add_dep("sp", rec, b.w)
                    for r in b.r:
                        self._add_dep("sp", rec, r)
                self.ops["sp"].append(rec)
            sigs = {e: [r["ord"] for r in self.ops[e] if r["signal"]] for e in self.ENGS}

            def resolve(dep):
                if dep[0] == "e":
                    _, e2, o2 = dep
                    j = bisect.bisect_left(sigs[e2], o2)
                    assert j < len(sigs[e2]), dep
                    return esem[e2], j + 1
                _, q, k, val = dep
                return dsem[(q, k)], val

            def make(ename):
                def f(eng):
                    for rec in self.ops[ename]:
                        for dep in rec["deps"]:
                            s, v = resolve(dep)
                            eng.wait_ge(s, v)
                        if rec["dma"] is not None:
                            k, out, in_ = rec["dma"]
                            eng.dma_start(out=out, in_=in_).then_inc(dsem[(ename, k)], 16)
                        elif rec["fn"] is not None:
                            ins = rec["fn"](eng)
                            if rec["signal"]:
                                ins.then_inc(esem[ename], 1)
                return f

            with nc.Block() as block:
                block.tensor(make("pe"))
                block.scalar(make("act"))
                block.vector(make("dve"))
                block.gpsimd(make("pool"))
                block.sync(make("sp"))
        self.stats = {e: (len(self.ops[e]), len(sigs[e])) for e in self.ENGS}


class Ring:
    def __init__(self, aps, name, bank=False):
        self.items = [(ap, Buf("%s%d" % (name, i), bank)) for i, ap in enumerate(aps)]
        self.i = 0

    def take(self):
        ap, old = self.items[self.i]
        new = Buf(old.name, old.bank)
        new.w, new.r = old.w, old.r
        old.dead = True
        self.items[self.i] = (ap, new)
        self.i = (self.i + 1) % len(self.items)
        return ap, new

    def bufs(self):
        return [b for _, b in self.items]


class Arena:
    def __init__(self, t, base, size):
        self.t, self.base, self.size, self.cur = t, base, size, 0

    def alloc(self, nbytes, dt=BF16):
        nbytes = (nbytes + 63) // 64 * 64
        assert self.cur + nbytes <= self.size, ("arena overflow", self.cur, nbytes, self.size)
        off = self.base + self.cur
        self.cur += nbytes
        ap = self.t[:, off // 2:(off + nbytes) // 2]
        if dt == F32:
            ap = ap.bitcast(F32)
        return ap


class _Stop(Exception):
    pass


def build_program(dbg=False, stage=None):
    nc = bass.Bass("TRN2", target_bir_lowering=False)
    S = Sched(nc)

    def checkpoint(n):
        if stage == n:
            S.emit(final_bufs=[OUTB])
            raise _Stop()

    def din(name, shape, dt=F32):
        return nc.dram_tensor(name, list(shape), dt, kind="ExternalInput").ap()

    x_d = din("x", [SEQ, D])
    ccol_d = din("ccol", [128, 8])
    wada_d = din("w_ada", [D, 6 * D]).rearrange("(kc p) n -> p kc n", p=128)
    badac_d = din("b_ada_col", [128, 6, 8])
    bada_d = din("b_ada", [6 * D])
    wprec_d = din("w_pre_col", [128, 8])
    wpreffnc_d = din("w_preffn_col", [128, 8])
    wpost_d = din("w_post", [D])
    wpostffn_d = din("w_postffn", [D])
    win_d = din("w_in", [D, DIN]).rearrange("(kc p) n -> p kc n", p=128)
    cw_d = din("dn_conv_col", [128, 12, 4])
    alog_d = din("dn_a_log", [NH])
    dtb_d = din("dn_dt_bias", [NH])
    nw_d = din("dn_norm_w", [DH]).rearrange("(p o) -> p o", o=1)
    scw_d = din("sc_conv_col", [128, 4, 3])
    wout_d = din("w_out", [D, D]).rearrange("(kc p) n -> p kc n", p=128)
    wff1_d = din("w_ff1", [D, DFF]).rearrange("(kc p) n -> p kc n", p=128)
    wff2_d = din("w_ff2", [DFF, D]).rearrange("(fc p) n -> p fc n", p=128)
    identb_d = din("ident_bf", [128, 128], BF16)
    tri_d = din("tri", [128, 128])
    pms_d = din("pm_strict", [128, 128])
    pmt_d = din("pm_t", [128, 128])
    lvl_d = din("lvlmaskT", [128, 7, 128], BF16)
    out_d = nc.dram_tensor("out", [SEQ, D], F32, kind="ExternalOutput").ap()
    dbg_outs = {}
    OUTB = Buf("out")

    def MM(out, lhsT, rhs, start=True, stop=True, rd=(), wr=()):
        S.op("pe", lambda e: e.matmul(out, lhsT=lhsT, rhs=rhs, start=start, stop=stop), rd, wr)

    def TR(out, in_, ident, rd=(), wr=()):
        S.op("pe", lambda e: e.transpose(out, in_, ident), rd, wr)

    def ACT(out, in_, func, bias=None, scale=None, accum=None, rd=(), wr=()):
        kw = {}
        if bias is not None:
            kw["bias"] = bias
        if scale is not None:
            kw["scale"] = scale
        if accum is not None:
            kw["accum_out"] = accum
        S.op("act", lambda e: e.activation(out=out, in_=in_, func=func, **kw), rd, wr)

    def TT(eng, out, in0, in1, op, rd=(), wr=()):
        S.op(eng, lambda e: e.tensor_tensor(out=out, in0=in0, in1=in1, op=op), rd, wr)

    def TS(eng, out, in0, s1, op0, s2=None, op1=None, rd=(), wr=()):
        if op1 is None:
            S.op(eng, lambda e: e.tensor_scalar(out=out, in0=in0, scalar1=s1, scalar2=None, op0=op0), rd, wr)
        else:
            S.op(eng, lambda e: e.tensor_scalar(out=out, in0=in0, scalar1=s1, scalar2=s2, op0=op0, op1=op1), rd, wr)

    def STT(out, in0, scalar, in1, op0, op1, rd=(), wr=()):
        S.op("dve", lambda e: e.scalar_tensor_tensor(out=out, in0=in0, scalar=scalar, in1=in1, op0=op0, op1=op1), rd, wr)

    def CP(eng, out, in_, rd=(), wr=()):
        if eng == "act":
            S.op("act", lambda e: e.copy(out=out, in_=in_), rd, wr)
        else:
            S.op(eng, lambda e: e.tensor_copy(out=out, in_=in_), rd, wr)

    def MEMSET(eng, ap, val, wr=()):
        S.op(eng, lambda e: e.memset(ap, val), (), wr)

    def RECIP(out, in_, rd=(), wr=()):
        S.op("dve", lambda e: e.reciprocal(out=out, in_=in_), rd, wr)

    def DMA(q, out, in_, rd=(), wr=()):
        S.dma(q, out, in_, rd, wr)

    def dump(name, ap, bufs, dt=F32):
        if not dbg:
            return
        shp = list(ap.shape)
        d = nc.dram_tensor("dbg_" + name, shp, dt, kind="ExternalOutput").ap()
        dbg_outs[name] = d
        DMA("sp", d, ap, rd=bufs, wr=[OUTB])

    try:
        with contextlib.ExitStack() as st:
            TOTAL = 212736
            arena_t = st.enter_context(nc.sbuf_tensor("arena", [128, TOTAL // 2], BF16))
            P_SZ, R1_SZ, R3_SZ, A_SZ = 16 * 1024, 64 * 1024, 32 * 1024, 64 * 1024
            BS_SZ = TOTAL - (P_SZ + R1_SZ + R3_SZ + A_SZ)
            AP_ = Arena(arena_t, 0, P_SZ)
            R1 = arena_t[:, P_SZ // 2:(P_SZ + R1_SZ) // 2]
            WIN_BYTES = 8 * DIN * 2
            AA2 = Arena(arena_t, P_SZ + WIN_BYTES, R1_SZ - WIN_BYTES)
            R3 = arena_t[:, (P_SZ + R1_SZ) // 2:(P_SZ + R1_SZ + R3_SZ) // 2]
            A_BASE = P_SZ + R1_SZ + R3_SZ
            RA = arena_t[:, A_BASE // 2:(A_BASE + A_SZ) // 2]
            AA = Arena(arena_t, A_BASE, A_SZ)
            BS_BASE = A_BASE + A_SZ
            AB = Arena(arena_t, BS_BASE, BS_SZ - 1024)

            banks_bf = [st.enter_context(nc.psum_tensor("bank%d" % i, [128, 1024], BF16))[:, :] for i in range(8)]
            pbanks = [b.bitcast(F32) for b in banks_bf[0:6]]
            ptbs = banks_bf[6:8]
            PT = Ring([t[:, :] for t in ptbs], "pt", bank=True)
            PB = Ring([pb[:, :] for pb in pbanks[0:2]], "pbp", bank=True)
            PD = Ring([pb[:, :] for pb in pbanks[2:6]], "pbd", bank=True)

            def pconst(nbytes, dt, name):
                return AP_.alloc(nbytes, dt), Buf(name)

            ident, identB = pconst(256, BF16, "ident")
            tri, triB = pconst(512, F32, "tri")
            onesb, onesbB = pconst(256, BF16, "onesb")
            onesf, onesfB = pconst(512, F32, "onesf")
            pms, pmsB = pconst(512, F32, "pms")
            pmt, pmtB = pconst(512, F32, "pmt")
            lvlT, lvlTB = pconst(7 * 256, BF16, "lvlT")
            lvlT = lvlT.rearrange("p (k n) -> p k n", k=7)
            nwbc, nwbcB = pconst(512, F32, "nwbc")
            smallc, smallB = pconst(64 * 4, F32, "smallc")
            dtb = smallc[:, 0:4]
            alog = smallc[:, 4:8]
            negA = smallc[:, 8:12]
            ccol = smallc[:, 16:24]
            cact = smallc[:, 24:32]
            cwt, cwB = pconst(12 * 4 * 4, F32, "cw")
            cwt = cwt.rearrange("p (c j) -> p c j", c=12)
            scwt, scwB = pconst(4 * 3 * 4, F32, "scw")
            scwt = scwt[:, 0:12].rearrange("p (c j) -> p c j", c=4)
            modc, modcB = pconst(4 * 8 * 4, F32, "modc")
            modc = modc.rearrange("p (v k) -> p v k", v=4)
            badac, badacB = pconst(6 * 8 * 4, F32, "badac")
            badac = badac.rearrange("p (v k) -> p v k", v=6)
            wnc, wncB = pconst(2 * 8 * 4, F32, "wnc")
            wnc = wnc.rearrange("p (v k) -> p v k", v=2)
            Gm, GmB = pconst(4096, F32, "Gm")
            Gf, GfB = pconst(4096, F32, "Gf")
            cactb, cactbB = pconst(8 * 2, BF16, "cactb")
            cactb = cactb[:, 0:8]
            crep_ap, crepB = pconst(2048, BF16, "crep")
            crep_ap = crep_ap.rearrange("p (k m) -> p k m", k=8)
            ysq, ysqB = pconst(128, F32, "ysq")
            print("persistent arena used", AP_.cur, "of", AP_.size)

            DMA("sp", ident, identb_d, wr=[identB])
            DMA("sp", tri, tri_d, wr=[triB])
            DMA("sp", pms, pms_d, wr=[pmsB])
            DMA("sp", pmt, pmt_d, wr=[pmtB])
            DMA("sp", lvlT, lvl_d, wr=[lvlTB])
            DMA("sp", nwbc[:, 0:1], nw_d, wr=[nwbcB])
            DMA("sp", dtb, dtb_d.partition_broadcast(128), wr=[smallB])
            DMA("sp", alog, alog_d.partition_broadcast(128), wr=[smallB])
            DMA("sp", ccol, ccol_d, wr=[smallB])
            DMA("sp", cwt, cw_d, wr=[cwB])
            DMA("sp", scwt, scw_d, wr=[scwB])
            DMA("sp", badac, badac_d, wr=[badacB])
            DMA("sp", wnc[:, 0, :], wprec_d, wr=[wncB])
            DMA("sp", wnc[:, 1, :], wpreffnc_d, wr=[wncB])
            MEMSET("dve", onesb, 1.0, wr=[onesbB])
            MEMSET("dve", onesf, 1.0, wr=[onesfB])
            ACT(negA, alog, AF.Exp, rd=[smallB], wr=[smallB])
            TS("dve", negA, negA, -1.0, ALU.mult, rd=[smallB], wr=[smallB])
            ACT(cact, ccol, AF.Exp, scale=-1.0, rd=[smallB], wr=[smallB])
            TS("dve", cact, cact, 1.0, ALU.add, rd=[smallB], wr=[smallB])
            RECIP(cact, cact, rd=[smallB], wr=[smallB])
            TT("dve", cact, ccol, cact, ALU.mult, rd=[smallB], wr=[smallB])
            CP("dve", cactb, cact, rd=[smallB], wr=[cactbB])
            dump("cact", cact, [smallB])
            checkpoint(11)

            win_s = R1[:, 0:8 * DIN].rearrange("p (k n) -> p k n", k=8)
            WIN_SL = [(0, 512), (512, 1024), (1024, 1536), (1536, 2056), (2056, 2568), (2568, 3080), (3080, 3592)]
            winB = [Buf("win%d" % i) for i in range(len(WIN_SL))]

            def win_buf(c0):
                for i, (a, b) in enumerate(WIN_SL):
                    if a <= c0 < b:
                        return winB[i]
                raise AssertionError

            wout_s = AB.alloc(16 * 1024).rearrange("p (k n) -> p k n", k=8)
            woutB = [Buf("wout0"), Buf("wout1")]

            wst = [R3[:, i * 4096:(i + 1) * 4096].rearrange("p (k n) -> p k n", k=8) for i in range(2)]
            wstB = [Buf("wst0"), Buf("wst1")]
            wst2 = arena_t[:, BS_BASE // 2:BS_BASE // 2 + 4096].rearrange("p (k n) -> p k n", k=8)
            wst2B = Buf("wst2")
            bst_ap = AB.alloc(4096, F32)
            bst = [bst_ap[:, 0:512], bst_ap[:, 512:1024]]
            bstB = [Buf("bst0"), Buf("bst1")]
            junk0 = Arena(arena_t, BS_BASE + BS_SZ - 1024, 1024).alloc(1024)
            junk0B = Buf("junk0")
            R3_stage_bufs = wstB
            B0_bufs = [junk0B]

            modcBf = Buf("modc_f")

            def ada_stage(j):
                return (wst[j % 2], wstB[j % 2]) if j < 4 else (wst2, wst2B)

            def ada_dma(j):
                stg, stgB = ada_stage(j)
                DMA("pool", stg, wada_d[:, :, j * 512:(j + 1) * 512], wr=[stgB])

            def ada_slice(j, ps, psB, dma=True):
                vec, half = j // 2, j % 2
                sl = slice(j * 512, (j + 1) * 512)
                stg, stgB = ada_stage(j)
                if dma:
                    ada_dma(j)
                if vec in (2, 5):
                    for kc in range(8):
                        MM(ps, crep_ap[:, kc, :], stg[:, kc, :], start=(kc == 0), stop=(kc == 7), rd=[crepB, stgB], wr=[psB])
                    b0, b0B = bst[0], bstB[0]
                    b1, b1B = bst[1], bstB[1]
                    DMA("sp", b0, bada_d[sl].partition_broadcast(128), wr=[b0B])
                    wsrc = wpost_d if vec == 2 else wpostffn_d
                    DMA("sp", b1, wsrc[half * 512:(half + 1) * 512].partition_broadcast(128), wr=[b1B])
                    G, GB = (Gm, GmB) if vec == 2 else (Gf, GfB)
                    Gs = G[:, half * 512:(half + 1) * 512]
                    TT("dve", Gs, ps, b0, ALU.add, rd=[psB, b0B], wr=[GB])
                    TT("dve", Gs, Gs, b1, ALU.mult, rd=[b1B, GB], wr=[GB])
                else:
                    for fb in range(4):
                        for kc in range(8):
                            MM(ps[:, fb:fb + 1], stg[:, kc, fb * 128:(fb + 1) * 128], cactb[:, kc:kc + 1],
                               start=(kc == 0), stop=(kc == 7), rd=[stgB, cactbB], wr=[psB])
                    ci = {0: 1, 1: 0, 3: 3, 4: 2}[vec]
                    mB = modcB if ci < 2 else modcBf
                    dst = modc[:, ci, half * 4:(half + 1) * 4]
                    TT("dve", dst, ps[:, 0:4], badac[:, vec, half * 4:(half + 1) * 4], ALU.add, rd=[psB, badacB], wr=[mB])
                    if vec in (1, 4):
                        wn = wnc[:, 0 if vec == 1 else 1, half * 4:(half + 1) * 4]
                        STT(dst, dst, 1.0, wn, ALU.add, ALU.mult, rd=[mB, wncB], wr=[mB])

            for kc in range(8):
                TS("dve", crep_ap[:, kc, :], onesb, cact[:, kc:kc + 1], ALU.mult, rd=[onesbB, smallB], wr=[crepB])


            for j in (0, 1, 2, 3):
                ps, psB = PB.take()
                ada_slice(j, ps, psB)
            dump("modc_a", modc.rearrange("p v k -> p (v k)"), [modcB])
            checkpoint(12)
            for i, (a, b) in enumerate(WIN_SL):
                DMA("pool", win_s[:, :, a:b], win_d[:, :, a:b], wr=[winB[i]])
            checkpoint(13)
            checkpoint(14)
            checkpoint(15)
            checkpoint(1)

            def ring_of(arena, n, nbytes, dt, name):
                return Ring([arena.alloc(nbytes, dt) for _ in range(n)], name)

            qkvT_all = AA.alloc(12 * SEQ * 2, BF16).rearrange("p (c n) -> p c n", c=12)
            qkvB_all = [[Buf("qkv%d_%d" % (i, c)) for c in range(12)] for i in range(NS)]
            gates_all = AA.alloc(NT * 32 * 4, F32).rearrange("p (t c) -> p t c", t=NT)
            gateB = [Buf("gate%d" % t) for t in range(NT)]
            halo = AA.alloc(12 * 3 * 4, F32)[:, 0:36].rearrange("p (c j) -> p c j", c=12)
            haloB = [Buf("halo%d" % c) for c in range(12)]
            halo2 = AA.alloc(64, F32)[:, 0:8].rearrange("p (c j) -> p c j", c=4)
            halo2B = [Buf("halo2_%d" % c) for c in range(4)]
            XIN = ring_of(AA2, 1, 4096, F32, "xin")
            ABy = Arena(arena_t, BS_BASE + 8192, 8192)
            XN = Ring([AA.alloc(2048, BF16), ABy.alloc(2048, BF16)], "xn")
            hTs = [AA.alloc(8 * STOK * 2, BF16).rearrange("p (k n) -> p k n", k=8),
                   ABy.alloc(8 * STOK * 2, BF16).rearrange("p (k n) -> p k n", k=8)]
            hTBs = [[Buf("hT%d_%d" % (i, t)) for t in range(ST)] for i in range(2)]
            hTB = hTBs[0] + hTBs[1]
            SCOL = ring_of(AA, 8, 64, F32, "scol")
            ABx = Arena(arena_t, BS_BASE + 20 * 1024, BS_SZ - 21 * 1024)
            PRE = Ring([AA.alloc((STOK + 3) * 4, F32) for _ in range(2)] + [ABx.alloc((STOK + 3) * 4, F32) for _ in range(3)], "pre")
            ACC = Ring([AA.alloc(STOK * 4, F32) for _ in range(2)] + [ABx.alloc(STOK * 4, F32) for _ in range(3)], "acc")
            FZ2 = ABx.alloc(2048, F32)
            PBA = Ring([pb[:, :] for pb in pbanks], "pba", bank=True)
            SCS = ring_of(AA, 1, STOK * 4, F32, "scs")
            PRE2 = ring_of(AA, 1, (STOK + 2) * 4, F32, "pre2")
            SQ = Ring([AA2.alloc(STOK * 2, BF16), ABx.alloc(STOK * 2, BF16)], "sq")
            RSTD = ring_of(AA2, 1, STOK * 4, F32, "rstd")
            FZ = Ring([AA2.alloc(2048, F32), FZ2], "fz")
            print("phase A1 arena used", AA.cur, "of", AA.size, "| tail", AA2.cur, "of", AA2.size)
            allA_rings = [XN, SCOL, PRE, ACC, SCS, PRE2]
            allA2_rings = [XIN, SQ, RSTD, FZ]

            for c in range(12):
                MEMSET("pool", halo[:, c, :], 0.0, wr=[haloB[c]])
            for c in range(4):
                MEMSET("pool", halo2[:, c, :], 0.0, wr=[halo2B[c]])
            checkpoint(16)
            ocatT = R3.rearrange("p (k n) -> p k n", k=8)
            ocatB = [Buf("ocat%d" % t) for t in range(NT)]
            alias_init(ocatB, R3_stage_bufs)

            def v4(ap):
                return ap.rearrange("p (h n) -> p h n", h=4)

            def bc_inner(col4):
                return col4.unsqueeze(2).to_broadcast([128, 4, 128])

            def bc_mid(m):
                return m.unsqueeze(1).to_broadcast([128, 4, 128])

            def rsqrt_col(dst, src, scale, eps, rd, wr):
                ACT(dst, src, AF.Ln, bias=eps, scale=scale, rd=rd, wr=wr)
                ACT(dst, dst, AF.Exp, scale=-0.5, rd=wr, wr=wr)

            def run_window(gens, W):
                gens = list(gens)
                active = []
                i = 0
                while active or i < len(gens):
                    if i < len(gens) and len(active) < W:
                        active.append(gens[i])
                        i += 1
                    for g in list(active):
                        try:
                            next(g)
                        except StopIteration:
                            active.remove(g)
                    yield


            tiles = {}

            def gen_ln_tile(s, t):
                tg = s * ST + t
                hTc, hTBc = hTs[s % 2], hTBs[s % 2]
                xin, xinB = XIN.take()
                DMA("sp", xin, x_d[tg * 128:(tg + 1) * 128, :], wr=[xinB])
                sc, scB = SCOL.take()
                xn, xnB = XN.take()
                ACT(xn, xin, AF.Square, accum=sc[:, 0:1], rd=[xinB], wr=[xnB, scB])
                rsqrt_col(sc[:, 1:2], sc[:, 0:1], 1.0 / D, EPS, [scB], [scB])
                ACT(xn, xin, AF.Identity, scale=sc[:, 1:2], rd=[xinB, scB], wr=[xnB])
                yield
                pt, ptB = PT.take()
                for kc in range(8):
                    TR(pt[:, kc * 128:(kc + 1) * 128], xn[:, kc * 128:(kc + 1) * 128], ident, rd=[xnB, identB], wr=[ptB])
                for kc in range(8):
                    dst = hTc[:, kc, t * 128:(t + 1) * 128]
                    src = pt[:, kc * 128:(kc + 1) * 128]
                    if kc < 4:
                        ACT(dst, src, AF.Identity, bias=modc[:, 1, kc:kc + 1], scale=modc[:, 0, kc:kc + 1],
                            rd=[ptB, modcB], wr=[hTBc[t]])
                    else:
                        TS("dve", dst, src, modc[:, 0, kc:kc + 1], ALU.mult, modc[:, 1, kc:kc + 1], ALU.add,
                           rd=[ptB, modcB], wr=[hTBc[t]])
                yield
                if s == 0 and t == ST - 1:
                    dump("hT0", hTc.rearrange("p k n -> p (k n)"), hTBc, BF16)

            def proj_items(s):
                qk, qkB = qkvT_all[:, :, s * STOK:(s + 1) * STOK], qkvB_all[s]
                hT, hB = hTs[s % 2], hTBs[s % 2]

                def proj_chunk(c0):
                    ps, psB = PBA.take()
                    for kc in range(8):
                        MM(ps[:, 0:STOK], win_s[:, kc, c0:c0 + 128], hT[:, kc, :], start=(kc == 0), stop=(kc == 7),
                           rd=[win_buf(c0)] + hB, wr=[psB])
                    return ps, psB

                def gen_chunk(ch):
                    ps, psB = proj_chunk(ch * 128)
                    pre, preB = PRE.take()
                    CP("act", pre[:, 0:3], halo[:, ch, :], rd=[haloB[ch]], wr=[preB])
                    CP("act", pre[:, 3:3 + STOK], ps[:, 0:STOK], rd=[psB], wr=[preB])
                    CP("act", halo[:, ch, :], pre[:, STOK:STOK + 3], rd=[preB], wr=[haloB[ch]])
                    yield
                    acc, accB = ACC.take()
                    TS("dve", acc, pre[:, 0:STOK], cwt[:, ch, 0:1], ALU.mult, rd=[preB, cwB], wr=[accB])
                    for j in (1, 2, 3):
                        STT(acc, pre[:, j:j + STOK], cwt[:, ch, j:j + 1], acc, ALU.mult, ALU.add, rd=[preB, cwB, accB], wr=[accB])
                    yield
                    ACT(qk[:, ch, :], acc, AF.Silu, rd=[accB], wr=[qkB[ch]])
                    yield
                    if s == 0 and ch == 11:
                        dump("qkv_silu0", qk, qkB, BF16)

                def gen_l2(ch):
                    sqs = []
                    for c in (ch, ch + 1):
                        sq, sqB = SQ.take()
                        ACT(sq, qk[:, c, :], AF.Square, rd=[qkB[c]], wr=[sqB])
                        sqs.append((sq, sqB))
                    yield
                    ps, psB = PBA.take()
                    for i2, (sq, sqB) in enumerate(sqs):
                        MM(ps[:, i2 * STOK:(i2 + 1) * STOK], onesb, sq, rd=[onesbB, sqB], wr=[psB])
                    rs, rsB = FZ.take()
                    ACT(rs, ps, AF.Ln, bias=EPS, rd=[psB], wr=[rsB])
                    ACT(rs, rs, AF.Exp, scale=-0.5, rd=[rsB], wr=[rsB])
                    yield
                    qv = qk[:, ch:ch + 2, :]
                    rv = rs.rearrange("p (c n) -> p c n", c=2)
                    if ch < 4:
                        STT(qv, qv, DH ** -0.5, rv, ALU.mult, ALU.mult, rd=[qkB[ch], qkB[ch + 1], rsB], wr=[qkB[ch], qkB[ch + 1]])
                    else:
                        TT("dve", qv, qv, rv, ALU.mult, rd=[qkB[ch], qkB[ch + 1], rsB], wr=[qkB[ch], qkB[ch + 1]])
                    yield
                    if s == 0 and ch == 6:
                        dump("qkv_n0", qk, qkB, BF16)

                def gen_z(t):
                    tg = s * ST + t
                    T = tiles.setdefault(tg, {})
                    tcols = slice(t * 128, (t + 1) * 128)
                    ps, psB = PBA.take()
                    for kc in range(8):
                        MM(ps, hT[:, kc, tcols], win_s[:, kc, 1536:2048], start=(kc == 0), stop=(kc == 7),
                           rd=[hB[t], win_buf(1536)], wr=[psB])
                    zg, zgB = ocatT[:, 0:4, tg * 128:(tg + 1) * 128], ocatB[tg]
                    ACT(zg, v4(ps), AF.Silu, rd=[psB], wr=[zgB])
                    T["zgw"] = (zg, zgB)
                    yield

                def gen_zg(t):
                    tg = s * ST + t
                    T = tiles.setdefault(tg, {})
                    tcols = slice(t * 128, (t + 1) * 128)
                    zg, zgB = T["zgw"]
                    ps, psB = PBA.take()
                    for kc in range(8):
                        MM(ps[:, 0:8], hT[:, kc, tcols], win_s[:, kc, 2048:2056], start=(kc == 0), stop=(kc == 7),
                           rd=[hB[t], win_buf(2048)], wr=[psB])
                    gt, gtB = gates_all[:, tg, :], gateB[tg]
                    wk, wkB = SCOL.take()
                    CP("act", wk[:, 0:8], ps[:, 0:8], rd=[psB], wr=[wkB])
                    T["gate"] = (gt, gtB)
                    T["qk"] = (qk, qkB, tcols)
                    if tg == 0:
                        dump("gate0", wk, [wkB])
                        dump("zgw0", zg, [zgB], BF16)
                    yield
                    xg, ax = wk[:, 8:12], wk[:, 12:16]
                    TT("dve", xg, wk[:, 0:4], dtb, ALU.add, rd=[wkB, smallB], wr=[wkB])
                    STT(ax, xg, -1.0, xg, ALU.mult, ALU.max, rd=[wkB], wr=[wkB])
                    yield
                    ACT(ax, ax, AF.Exp, scale=-1.0, rd=[wkB], wr=[wkB])
                    ACT(ax, ax, AF.Ln, bias=1.0, rd=[wkB], wr=[wkB])
                    ACT(wk[:, 4:8], wk[:, 4:8], AF.Exp, scale=-1.0, rd=[wkB], wr=[wkB])
                    ACT(wk[:, 4:8], wk[:, 4:8], AF.Ln, bias=1.0, rd=[wkB], wr=[wkB])
                    ACT(gt[:, 4:8], wk[:, 4:8], AF.Exp, scale=-1.0, rd=[wkB], wr=[gtB])
                    yield
                    STT(xg, xg, 0.0, ax, ALU.max, ALU.add, rd=[wkB], wr=[wkB])
                    TT("dve", gt[:, 0:4], xg, negA, ALU.mult, rd=[wkB, smallB], wr=[gtB])
                    yield
                    ps, psB = PBA.take()
                    MM(ps[:, 0:4], tri, gt[:, 0:4], rd=[triB, gtB], wr=[psB])
                    MM(ps[:, 4:8], onesf, gt[:, 0:4], rd=[onesfB, gtB], wr=[psB])
                    CP("act", gt[:, 8:16], ps[:, 0:8], rd=[psB], wr=[gtB])
                    yield
                    TS("dve", gt[:, 16:20], gt[:, 8:12], -1.0, ALU.mult, rd=[gtB], wr=[gtB])
                    TT("dve", wk[:, 0:4], gt[:, 12:16], gt[:, 8:12], ALU.subtract, rd=[gtB], wr=[wkB])
                    yield
                    ACT(gt[:, 20:28], gt[:, 8:16], AF.Exp, rd=[gtB], wr=[gtB])
                    ACT(gt[:, 28:32], wk[:, 0:4], AF.Exp, rd=[wkB], wr=[gtB])
                    yield

                def gen_sc(i):
                    psC, psCB = proj_chunk(2568 + i * 128)
                    scs, scsB = SCS.take()
                    CP("act", scs, psC[:, 0:STOK], rd=[psCB], wr=[scsB])
                    yield
                    psH, psHB = proj_chunk(3080 + i * 128)
                    pre2, pre2B = PRE2.take()
                    CP("act", pre2[:, 0:2], halo2[:, i, :], rd=[halo2B[i]], wr=[pre2B])
                    TT("dve", pre2[:, 2:2 + STOK], psH[:, 0:STOK], scs, ALU.mult, rd=[psHB, scsB], wr=[pre2B])
                    CP("act", halo2[:, i, :], pre2[:, STOK:STOK + 2], rd=[pre2B], wr=[halo2B[i]])
                    yield
                    acc, accB = ACC.take()
                    TS("dve", acc, pre2[:, 0:STOK], scwt[:, i, 0:1], ALU.mult, rd=[pre2B, scwB], wr=[accB])
                    for j in (1, 2):
                        STT(acc, pre2[:, j:j + STOK], scwt[:, i, j:j + 1], acc, ALU.mult, ALU.add, rd=[pre2B, scwB, accB], wr=[accB])
                    yield
                    psBb, psBB = proj_chunk(2056 + i * 128)
                    tg0 = s * ST
                    TT("dve", ocatT[:, 4 + i, tg0 * 128:(tg0 + ST) * 128], psBb[:, 0:STOK], acc, ALU.mult,
                       rd=[psBB, accB], wr=[ocatB[tg0 + t2] for t2 in range(ST)])
                    yield

                items = [gen_chunk(ch) for ch in range(12)] + [gen_z(t) for t in range(ST)]
                L2_CH = (0, 2, 4, 6)
                if s + 1 < NS:
                    items += [gen_ln_tile(s + 1, t) for t in range(ST)]
                items += [gen_l2(ch) for ch in L2_CH] + [gen_zg(t) for t in range(ST)] + [gen_sc(i) for i in range(4)]
                return items

            def gen_dn(tg, Bs):
                T = tiles[tg]
                PDr = Bs["PD"]

                class _PDF:
                    @staticmethod
                    def take():
                        ap, bf_ = PDr.take()
                        return ap.bitcast(F32), bf_
                PD = _PDF
                PT = PDr
                FD, UU, GREP, BT, SCOL = Bs["FD"], Bs["UU"], Bs["GREP"], Bs["BT"], SCOL2
                qk, qkB, tcols = T["qk"]
                gt, gtB = T["gate"]
                zg, zgB = T["zgw"]
                qTB = [qkB[h] for h in range(4)]
                kTB = [qkB[4 + h] for h in range(4)]
                g4, beta = gt[:, 0:4], gt[:, 4:8]
                gcB, ecB = gtB, gtB
                gc_map = {(0, 4): gt[:, 8:12], (4, 8): gt[:, 12:16], (8, 12): gt[:, 16:20]}
                ec_map = {(0, 4): gt[:, 20:24], (4, 8): gt[:, 24:28], (8, 12): gt[:, 28:32]}
                grep, grepB = GREP.take()
                TT("dve", v4(grep), bc_mid(onesf), bc_inner(g4), ALU.mult, rd=[onesfB, gtB], wr=[grepB])
                yield
                psg, psgB = PD.take()
                for h in range(NH):
                    MM(psg[:, h * 128:(h + 1) * 128], grep[:, h * 128:(h + 1) * 128], tri, rd=[grepB, triB], wr=[psgB])
                yield
                pt, ptB = PT.take()
                for i2, base in enumerate((4, 8)):
                    for h in range(NH):
                        TR(pt[:, (i2 * 4 + h) * 128:(i2 * 4 + h + 1) * 128], qk[:, base + h, tcols], ident,
                           rd=[qkB[base + h], identB], wr=[ptB])
                ke, keB = Bs["ke"].take()
                kdec, kdecB = Bs["kdec"].take()
                vtok, vtokB = Bs["VTOK"].take()
                CP("act", kdec, pt[:, 0:512], rd=[ptB], wr=[kdecB])
                CP("dve", vtok, pt[:, 512:1024], rd=[ptB], wr=[vtokB])
                yield
                TT("dve", v4(ke), v4(kdec), bc_inner(ec_map[(0, 4)]), ALU.mult, rd=[kdecB, ecB], wr=[keB])
                TT("dve", v4(kdec), v4(kdec), bc_inner(ec_map[(8, 12)]), ALU.mult, rd=[kdecB, ecB], wr=[kdecB])
                yield
                a1, a1B = FD.take()
                TT("dve", v4(a1), v4(psg), bc_mid(pms), ALU.add, rd=[psgB, pmsB], wr=[a1B])
                a2, a2B = FD.take()
                TT("dve", v4(a2), v4(psg), bc_mid(pmt), ALU.subtract, rd=[psgB, pmtB], wr=[a2B])
                yield
                egr, egrB = BT.take()
                ACT(egr, psg, AF.Exp, rd=[psgB], wr=[egrB])
                TT("dve", v4(a1), v4(a1), bc_inner(gc_map[(0, 4)]), ALU.subtract, rd=[a1B, gcB], wr=[a1B])
                TT("dve", v4(a2), v4(a2), bc_inner(gc_map[(0, 4)]), ALU.subtract, rd=[a2B, gcB], wr=[a2B])
                yield
                Dm, DmB = BT.take()
                ACT(Dm, a1, AF.Exp, scale=-1.0, rd=[a1B], wr=[DmB])
                DT, DTB = BT.take()
                ACT(DT, a2, AF.Exp, rd=[a2B], wr=[DTB])
                psk, pskB = PD.take()
                for h in range(NH):
                    MM(psk[:, h * 128:(h + 1) * 128], qk[:, 4 + h, tcols], qk[:, 4 + h, tcols], rd=[kTB[h]], wr=[pskB])
                yield
                Am, AmB = Bs["Am"].take()
                TT("dve", Am, psk, Dm, ALU.mult, rd=[pskB, DmB], wr=[AmB])
                psq, psqB = PD.take()
                for h in range(NH):
                    MM(psq[:, h * 128:(h + 1) * 128], qk[:, 4 + h, tcols], qk[:, h, tcols], rd=[kTB[h], qTB[h]], wr=[psqB])
                yield
                attT, attTB = Bs["attT"].take()
                TT("dve", attT, psq, DT, ALU.mult, rd=[psqB, DTB], wr=[attTB])
                qeT, qeTB = Bs["qeT"].take()
                TT("dve", v4(qeT), qk[:, 0:4, tcols], v4(egr), ALU.mult, rd=qTB + [egrB], wr=[qeTB])
                yield
                D0, D0B = Bs["Y"].take()
                TT("dve", v4(D0), bc_mid(ident), bc_inner(beta), ALU.mult, rd=[identB, gtB], wr=[D0B])
                yield
                X, XB = D0, D0B
                W, WB = D0, D0B
                Xn, XnB = Bs["X"].take()
                Wn, WnB = Bs["W"].take()
                for k in range(7):
                    psy, psyB = PD.take()
                    for h in range(NH):
                        hs = slice(h * 128, (h + 1) * 128)
                        MM(psy[:, hs], Am[:, hs], W[:, hs], rd=[AmB, WB], wr=[psyB])
                    yield
                    Y, YB = Bs["Bk"].take()
                    STT(v4(Y), v4(psy), -1.0, bc_mid(lvlT[:, k, :]), ALU.mult, ALU.mult, rd=[psyB, lvlTB], wr=[YB])
                    yield
                    psz, pszB = PD.take()
                    for h in range(NH):
                        hs = slice(h * 128, (h + 1) * 128)
                        MM(psz[:, hs], X[:, hs], Y[:, hs], start=True, stop=False, rd=[XB, YB], wr=[pszB])
                        MM(psz[:, hs], ident, W[:, hs], start=False, stop=True, rd=[identB, WB], wr=[pszB])
                    if k < 6:
                        pszt, psztB = PD.take()
                        for h in range(NH):
                            hs = slice(h * 128, (h + 1) * 128)
                            MM(pszt[:, hs], Y[:, hs], X[:, hs], start=True, stop=False, rd=[XB, YB], wr=[psztB])
                            MM(pszt[:, hs], ident, X[:, hs], start=False, stop=True, rd=[identB, XB], wr=[psztB])
                    yield
                    CP("act", Wn, psz, rd=[pszB], wr=[WnB])
                    W, WB = Wn, WnB
                    if k < 6:
                        CP("act", Xn, pszt, rd=[psztB], wr=[XnB])
                        X, XB = Xn, XnB
                    yield
                if tg == 0:
                    dump("W0", W, [WB], BF16)
                    dump("A0", Am, [AmB], BF16)
                psu, psuB = PD.take()
                for h in range(NH):
                    hs = slice(h * 128, (h + 1) * 128)
                    MM(psu[:, hs], W[:, hs], vtok[:, hs], rd=[WB, vtokB], wr=[psuB])
                psw, pswB = PD.take()
                for h in range(NH):
                    hs = slice(h * 128, (h + 1) * 128)
                    MM(psw[:, hs], ke[:, hs], W[:, hs], rd=[keB, WB], wr=[pswB])
                yield
                u, uB = UU.take()
                CP("act", u, psu, rd=[psuB], wr=[uB])
                wT, wTB = Bs["wT"].take()
                CP("act", wT, psw, rd=[pswB], wr=[wTB])
                yield
                ps1, ps1B = PD.take()
                for h in range(NH):
                    hs = slice(h * 128, (h + 1) * 128)
                    MM(ps1[:, hs], wT[:, hs], Sbf[:, hs], rd=[wTB, SbfB], wr=[ps1B])
                vn, vnB = Bs["vn"].take()
                TT("dve", vn, u, ps1, ALU.subtract, rd=[uB, ps1B], wr=[vnB])
                pso, psoB = PD.take()
                for h in range(NH):
                    hs = slice(h * 128, (h + 1) * 128)
                    MM(pso[:, hs], qeT[:, hs], Sbf[:, hs], start=True, stop=False, rd=[qeTB, SbfB], wr=[psoB])
                    MM(pso[:, hs], attT[:, hs], vn[:, hs], start=False, stop=True, rd=[attTB, vnB], wr=[psoB])
                ps3, ps3B = PD.take()
                for h in range(NH):
                    hs = slice(h * 128, (h + 1) * 128)
                    MM(ps3[:, hs], kdec[:, hs], vn[:, hs], rd=[kdecB, vnB], wr=[ps3B])
                TT("dve", v4(Sst), v4(Sst), bc_inner(ec_map[(4, 8)]), ALU.mult, rd=[SstB, ecB], wr=[SstB])
                TT("dve", Sst, Sst, ps3, ALU.add, rd=[SstB, ps3B], wr=[SstB])
                CP("act", Sbf, Sst, rd=[SstB], wr=[SbfB])
                oc, ocB = SCOL.take()
                on, onB = Bs["odn"].take()
                for h in range(NH):
                    ACT(on[:, h * 128:(h + 1) * 128], pso[:, h * 128:(h + 1) * 128], AF.Square, accum=oc[:, h:h + 1],
                        rd=[psoB], wr=[onB, ocB])
                yield
                rsqrt_col(oc[:, 4:8], oc[:, 0:4], 1.0 / DH, EPS, [ocB], [ocB])
                TT("dve", v4(on), v4(pso), bc_inner(oc[:, 4:8]), ALU.mult, rd=[psoB, ocB], wr=[onB])
                yield
                TT("dve", v4(on), v4(on), zg, ALU.mult, rd=[onB, zgB], wr=[onB])
                yield
                pt, ptB = PT.take()
                for h in range(NH):
                    TR(pt[:, h * 128:(h + 1) * 128], on[:, h * 128:(h + 1) * 128], ident, rd=[onB, identB], wr=[ptB])
                CP("act", ocatT[:, 0:4, tg * 128:(tg + 1) * 128], v4(pt[:, 0:512]), rd=[ptB], wr=[ocatB[tg]])
                if tg == 0:
                    dump("odn0", on, [onB], BF16)
                    dump("S0", Sst, [SstB])
                yield

            step_ctr = [0]

            def run_interleaved(gens):
                gens = [g for g in gens if g is not None]
                while gens:
                    for g in list(gens):
                        try:
                            next(g)
                        except StopIteration:
                            gens.remove(g)
                        step_ctr[0] += 1
                        checkpoint(1000 + step_ctr[0])

            def chain(*gs):
                for g in gs:
                    yield from g

            alias_init(PBA.bufs(), PB.bufs() + PD.bufs())
            def gen_ada(j):
                ada_dma(j)
                for _ in range(10):
                    yield
                ps, psB = PBA.take()
                ada_slice(j, ps, psB, dma=False)
                yield

            a1_items = [gen_ln_tile(0, t) for t in range(ST)]
            for s in range(NS):
                a1_items += proj_items(s)
                a1_items.append(gen_ada(4 + s))
            for _ in run_window(a1_items, 11):
                step_ctr[0] += 1
                checkpoint(2000 + step_ctr[0])

            dump("modc", modc.rearrange("p v k -> p (v k)"), [modcB, modcBf])
            dump("Gm", Gm, [GmB])
            dump("Gf", Gf, [GfB])
            AR1 = Arena(arena_t, P_SZ, R1_SZ)
            ABd = Arena(arena_t, BS_BASE, BS_SZ - 1024)
            Sst = AR1.alloc(2048, F32)
            SstB = Buf("S")
            Sbf = AR1.alloc(1024, BF16)
            SbfB = Buf("Sbf")
            SCOL2 = ring_of(AR1, 16, 64, F32, "scol2")

            def make_dn_set(ar, pd, tag):
                Bs = dict(
                    VTOK=ring_of(ar, 1, 1024, BF16, "vtok" + tag),
                    FD=ring_of(ar, 2, 2048, F32, "fd" + tag), UU=ring_of(ar, 1, 2048, F32, "uu" + tag),
                    GREP=ring_of(ar, 1, 2048, F32, "grep" + tag), BT=ring_of(ar, 3, 1024, BF16, "bt" + tag),
                    PD=Ring([pb for pb in pd], "pbd" + tag, bank=True))
                for nm, n in (("Am", 1), ("attT", 1), ("qeT", 1), ("ke", 1), ("kdec", 1), ("X", 1), ("W", 1), ("Bk", 2),
                              ("Y", 1), ("wT", 1), ("vn", 1), ("odn", 1)):
                    Bs[nm] = ring_of(ar, n, 1024, BF16, nm + tag)
                return Bs

            class MultiArena:
                def __init__(self, arenas):
                    self.arenas = arenas

                def alloc(self, nbytes, dt=BF16):
                    need = (nbytes + 63) // 64 * 64
                    for a_ in self.arenas:
                        if a_.cur + need <= a_.size:
                            return a_.alloc(nbytes, dt)
                    raise AssertionError("multi-arena overflow")

            AA3 = Arena(arena_t, A_BASE + 12 * SEQ * 2 + NT * 32 * 4, AA.cur - (12 * SEQ * 2 + NT * 32 * 4))
            DNS = [make_dn_set(AR1, banks_bf[0:2], "a"), make_dn_set(AR1, banks_bf[2:4], "b"), make_dn_set(ABd, banks_bf[4:6], "c")]
            DNS.append(make_dn_set(MultiArena([AA3, AR1, ABd]), banks_bf[6:8], "d"))
            dn_rings = lambda Bs: [v for k, v in Bs.items() if k != "PD"]
            print("phase A2: R1 used", AR1.cur, "of", AR1.size, "| set c", ABd.cur, "of", ABd.size)
            r1_bufs_now = lambda: ([SstB, SbfB] + SCOL2.bufs() + sum([r.bufs() for r in dn_rings(DNS[0]) + dn_rings(DNS[1])], [])
                                   + winB + sum([r.bufs() for r in allA2_rings], []))
            r1_bufs = [SstB, SbfB] + SCOL2.bufs() + sum([r.bufs() for r in dn_rings(DNS[0]) + dn_rings(DNS[1])], [])
            alias_init(r1_bufs, winB + sum([r.bufs() for r in allA2_rings], []))
            bsx_bufs = PRE.bufs() + ACC.bufs() + FZ.bufs() + SQ.bufs() + XN.bufs() + hTB
            alias_init(sum([r.bufs() for r in dn_rings(DNS[2])], []), [wst2B] + bstB + bsx_bufs)
            for i in range(3):
                alias_init(DNS[i]["PD"].bufs(), PBA.bufs())
            alias_init(DNS[3]["PD"].bufs(), PT.bufs())
            old_everything = (winB + sum([r.bufs() for r in allA2_rings + allA_rings], []) + haloB + halo2B + hTB + bsx_bufs
                              + [wst2B] + bstB)
            set_d_bufs = lambda: sum([r.bufs() for r in dn_rings(DNS[3])], [])
            alias_init(set_d_bufs(), old_everything)
            MEMSET("pool", Sst, 0.0, wr=[SstB])
            MEMSET("pool", Sbf, 0.0, wr=[SbfB])
            active = []
            nxt_tile = 0
            while active or nxt_tile < NT:
                if nxt_tile < NT and len(active) < 4:
                    active.append(gen_dn(nxt_tile, DNS[nxt_tile % 4]))
                    nxt_tile += 1
                for g in list(active):
                    try:
                        next(g)
                    except StopIteration:
                        active.remove(g)
                step_ctr[0] += 1
                checkpoint(1000 + step_ctr[0])
            dump("ocat", ocatT[:, :, 0:256], ocatB[0:2], BF16)
            checkpoint(4)

            alias_init(PD.bufs(), DNS[1]["PD"].bufs() + DNS[2]["PD"].bufs())
            alias_init(PB.bufs(), DNS[0]["PD"].bufs())
            alias_init(PT.bufs(), DNS[3]["PD"].bufs())
            alias_init(woutB, sum([r.bufs() for r in dn_rings(DNS[2])], []) + [wst2B] + bstB + XN.bufs() + hTB)
            for i in range(2):
                DMA("pool", wout_s[:, :, i * 512:(i + 1) * 512], wout_d[:, :, i * 512:(i + 1) * 512], wr=[woutB[i]])
            for i in range(2):
                TS("dve", wout_s[:, 0:4, i * 512:(i + 1) * 512], wout_s[:, 0:4, i * 512:(i + 1) * 512], nwbc[:, 0:1], ALU.mult,
                   rd=[nwbcB], wr=[woutB[i]])
            ff1_s = RA.rearrange("p (k n) -> p k n", k=8)
            ff1B = [Buf("ff1_%d" % i) for i in range(8)]
            allA = (haloB + halo2B + hTB + gateB + [b for q in qkvB_all for b in q] + sum([r.bufs() for r in allA_rings], [])
                    + set_d_bufs())
            alias_init(ff1B, allA)
            for i in range(8):
                DMA("pool", ff1_s[:, :, i * 512:(i + 1) * 512], wff1_d[:, :, i * 512:(i + 1) * 512], wr=[ff1B[i]])
            ff2_s = R1.rearrange("p (k n) -> p k n", k=32)
            ff2B = [Buf("ff2_%d" % i) for i in range(8)]
            alias_init(ff2B, r1_bufs_now() + set_d_bufs())
            for i in range(8):
                DMA("pool", ff2_s[:, i * 4:(i + 1) * 4, :], wff2_d[:, i * 4:(i + 1) * 4, :], wr=[ff2B[i]])

            ABlo = Arena(arena_t, BS_BASE, 16 * 1024)
            ABhi = Arena(arena_t, BS_BASE + 16 * 1024, BS_SZ - 17 * 1024)
            ABj = Arena(arena_t, BS_BASE + BS_SZ - 1024, 1024)
            TB = 2
            NG = NT // TB
            NHS = 8
            xb_hi = [ABhi.alloc(4096, F32) for _ in range(2)]
            xb_lo = [ABlo.alloc(4096, F32) for _ in range(2)]
            XB_ = Ring(xb_hi + xb_lo, "xb")
            XN2 = Ring([ABhi.alloc(2048, BF16)], "xn2")
            h2Ts = [ABhi.alloc(8 * TB * 128 * 2, BF16).rearrange("p (k n) -> p k n", k=8),
                    ABlo.alloc(8 * TB * 128 * 2, BF16).rearrange("p (k n) -> p k n", k=8)]
            h2Bs = [[[Buf("h2_%d_%d_%d" % (i, t, k)) for k in range(8)] for t in range(TB)] for i in range(2)]
            hidT = ABlo.alloc(NHS * TB * 128 * 2, BF16).rearrange("p (k n) -> p k n", k=NHS)
            hidB = [[Buf("hid_%d_%d" % (t, k)) for k in range(NHS)] for t in range(TB)]
            BCOL = Ring([ABhi.alloc(64, F32) for _ in range(4)], "bcol")
            RTMP = Ring([ABj.alloc(128 * 4, F32) for _ in range(2)], "rtmp")
            print("phase B arenas used: lo", ABlo.cur, "of", ABlo.size, "| hi", ABhi.cur, "of", ABhi.size)
            hi_bufs = XB_.bufs()[0:2] + XN2.bufs() + [b for r in h2Bs[0] for b in r] + BCOL.bufs()
            lo_bufs = XB_.bufs()[2:4] + [b for r in h2Bs[1] for b in r] + [b for r in hidB for b in r] + RTMP.bufs()
            dead_after_dn = sum([r.bufs() for r in dn_rings(DNS[2])], []) + set_d_bufs() + bstB + [wst2B] + bsx_bufs
            alias_init(hi_bufs, dead_after_dn)

            def y_view(tg):
                return ocatT[:, :, tg * 128:(tg + 1) * 128]

            xbs_of = {}

            def gen_lnb(gI):
                h2T, h2B = h2Ts[gI % 2], h2Bs[gI % 2]
                for t in range(TB):
                    tg = gI * TB + t
                    xb, xbB = XB_.take()
                    xbs_of[(gI, t)] = (xb, xbB)
                    DMA("sp", xb, x_d[tg * 128:(tg + 1) * 128, :], wr=[xbB])
                    bc, bcB = BCOL.take()
                    yv = y_view(tg)
                    TT("dve", bc[:, 0:1], ysq[:, tg * 2:tg * 2 + 1], ysq[:, tg * 2 + 1:tg * 2 + 2], ALU.add, rd=[ysqB], wr=[bcB])
                    yield
                    rsqrt_col(bc[:, 1:2], bc[:, 0:1], 1.0 / D, EPS, [bcB], [bcB])
                    yield
                    xn2, xn2B = XN2.take()
                    tmpf = xn2.bitcast(F32)
                    for nh in range(2):
                        STT(v4(tmpf), yv[:, nh * 4:(nh + 1) * 4, :], bc[:, 1:2], v4(Gm[:, nh * 512:(nh + 1) * 512]), ALU.mult, ALU.mult,
                            rd=[ocatB[tg], bcB, GmB], wr=[xn2B])
                        TT("dve", xb[:, nh * 512:(nh + 1) * 512], xb[:, nh * 512:(nh + 1) * 512], tmpf, ALU.add, rd=[xbB, xn2B], wr=[xbB])
                        yield
                    ACT(xn2, xb, AF.Square, accum=bc[:, 2:3], rd=[xbB], wr=[xn2B, bcB])
                    yield
                    rsqrt_col(bc[:, 3:4], bc[:, 2:3], 1.0 / D, EPS, [bcB], [bcB])
                    yield
                    ACT(xn2, xb, AF.Identity, scale=bc[:, 3:4], rd=[xbB, bcB], wr=[xn2B])
                    if tg == 0:
                        dump("x1_0", xb, [xbB])
                    yield
                    pt, ptB = PT.take()
                    for kc in range(8):
                        TR(pt[:, kc * 128:(kc + 1) * 128], xn2[:, kc * 128:(kc + 1) * 128], ident, rd=[xn2B, identB], wr=[ptB])
                    for kc in range(8):
                        dst = h2T[:, kc, t * 128:(t + 1) * 128]
                        src = pt[:, kc * 128:(kc + 1) * 128]
                        if kc < 4:
                            ACT(dst, src, AF.Identity, bias=modc[:, 3, kc:kc + 1], scale=modc[:, 2, kc:kc + 1],
                                rd=[ptB, modcBf], wr=[h2B[t][kc]])
                        else:
                            TS("dve", dst, src, modc[:, 2, kc:kc + 1], ALU.mult, modc[:, 3, kc:kc + 1], ALU.add,
                               rd=[ptB, modcBf], wr=[h2B[t][kc]])
                    yield

            pss_of = {}

            def gen_ffn(gI):
                h2T, h2B = h2Ts[gI % 2], h2Bs[gI % 2]
                for hh in range(32 // NHS):
                    for f in range(NHS):
                        fc = hh * NHS + f
                        ps, psB = PB.take()
                        for kc in range(8):
                            MM(ps[:, 0:TB * 128], ff1_s[:, kc, fc * 128:(fc + 1) * 128], h2T[:, kc, :], start=(kc == 0), stop=(kc == 7),
                               rd=[ff1B[fc // 4]] + [h2B[t][kc] for t in range(TB)], wr=[psB])
                        for t in range(TB):
                            rt, rtB = RTMP.take()
                            ACT(rt, ps[:, t * 128:(t + 1) * 128], AF.Relu, rd=[psB], wr=[rtB])
                            TT("dve", hidT[:, f, t * 128:(t + 1) * 128], rt, rt, ALU.mult, rd=[rtB], wr=[hidB[t][f]])
                        yield
                    if hh == 0:
                        pss_of[gI] = [[PD.take(), PD.take()] for t in range(TB)]
                    pss = pss_of[gI]
                    for t in range(TB):
                        for f in range(NHS):
                            fc = hh * NHS + f
                            for nh in range(2):
                                ps, psB = pss[t][nh]
                                MM(ps, hidT[:, f, t * 128:(t + 1) * 128], ff2_s[:, fc, nh * 512:(nh + 1) * 512], start=(fc == 0), stop=(fc == 31),
                                   rd=[hidB[t][f], ff2B[fc // 4]], wr=[psB])
                            if f % 4 == 3:
                                yield

            def gen_epi(gI):
                pss = pss_of[gI]
                for t in range(TB):
                    tg = gI * TB + t
                    xb, xbB = xbs_of.pop((gI, t))
                    bc, bcB = BCOL.take()
                    jn = h2Ts[gI % 2][:, 0:4, t * 128:(t + 1) * 128]
                    jnB = [h2Bs[gI % 2][t][k] for k in range(4)]
                    for nh in range(2):
                        ps, psB = pss[t][nh]
                        ACT(jn, v4(ps), AF.Square, accum=bc[:, nh:nh + 1], rd=[psB], wr=jnB + [bcB])
                    yield
                    TT("dve", bc[:, 2:3], bc[:, 0:1], bc[:, 1:2], ALU.add, rd=[bcB], wr=[bcB])
                    yield
                    rsqrt_col(bc[:, 3:4], bc[:, 2:3], 1.0 / D, EPS, [bcB], [bcB])
                    yield
                    for nh in range(2):
                        ps, psB = pss[t][nh]
                        STT(ps, ps, bc[:, 3:4], Gf[:, nh * 512:(nh + 1) * 512], ALU.mult, ALU.mult, rd=[psB, bcB, GfB], wr=[psB])
                        yield
                        TT("dve", xb[:, nh * 512:(nh + 1) * 512], xb[:, nh * 512:(nh + 1) * 512], ps, ALU.add, rd=[xbB, psB], wr=[xbB])
                        yield
                    DMA("sp", out_d[tg * 128:(tg + 1) * 128, :], xb, rd=[xbB], wr=[OUTB])
                    yield

            def drain(g):
                for _ in g:
                    pass

            def step(g, n=1):
                for _ in range(n):
                    try:
                        next(g)
                    except StopIteration:
                        return False
                return True

            ln0 = gen_lnb(0)
            ln0_alive = True
            for tg in range(NT):
                tc = slice(tg * 128, (tg + 1) * 128)
                pss = [PD.take(), PD.take()]
                for nh in range(2):
                    ps, psB = pss[nh]
                    for kc in range(8):
                        MM(ps, ocatT[:, kc, tc], wout_s[:, kc, nh * 512:(nh + 1) * 512], start=(kc == 0), stop=(kc == 7),
                           rd=[ocatB[tg], woutB[nh]], wr=[psB])
                for nh in range(2):
                    ps, psB = pss[nh]
                    ACT(junk0, ps, AF.Square, accum=ysq[:, tg * 2 + nh:tg * 2 + nh + 1], rd=[psB], wr=[junk0B, ysqB])
                    CP("dve", ocatT[:, nh * 4:(nh + 1) * 4, tc], v4(ps), rd=[psB], wr=[ocatB[tg]])
                if tg >= TB and ln0_alive:
                    ln0_alive = step(ln0, 2)
            drain(ln0)
            alias_init(lo_bufs, woutB + B0_bufs + dead_after_dn)
            dump("y0", ocatT[:, :, 0:128], [ocatB[0]], BF16)
            checkpoint(5)

            NG = NT // TB
            for gI in range(NG):
                main = gen_ffn(gI)
                side = chain(gen_epi(gI - 1) if gI > 0 else iter(()), gen_lnb(gI + 1) if gI + 1 < NG else iter(()))
                rounds = 0
                main_alive, side_alive = True, True
                while main_alive:
                    main_alive = step(main)
                    if side_alive:
                        side_alive = step(side, 2 if rounds < NHS - 1 else 1)
                    rounds += 1
                drain(side)
            drain(gen_epi(NG - 1))

            S.emit(final_bufs=[OUTB])
    except _Stop:
        pass
    print("ops (total, signalling):", S.stats)
    return nc, dbg_outs


def _consts():
    i = np.arange(128)
    ident = np.eye(128, dtype=np.float32).astype(ml_dtypes.bfloat16)
    tri = (i[:, None] <= i[None, :]).astype(np.float32)
    pm_strict = np.where(i[None, :] < i[:, None], 0.0, BIG).astype(np.float32)
    pm_t = np.where(i[None, :] >= i[:, None], 0.0, BIG).astype(np.float32)
    lv = np.zeros((128, 7, 128), np.float32)
    for k in range(7):
        s = 1 << k
        same = (i[:, None] // (2 * s)) == (i[None, :] // (2 * s))
        up = (i[:, None] % (2 * s)) >= s
        lo = (i[None, :] % (2 * s)) < s
        lv[:, k, :] = (same & up & lo).astype(np.float32).T
    return dict(ident_bf=ident, tri=tri, pm_strict=pm_strict, pm_t=pm_t, lvlmaskT=lv.astype(ml_dtypes.bfloat16))


def make_in_maps(x, c, w_ada, b_ada, pre_mix_norm_w, post_mix_norm_w, w_in, dn_conv_w, dn_a_log, dn_dt_bias,
                 dn_norm_w, sc_conv_w, w_out, pre_ffn_norm_w, post_ffn_norm_w, w_ff1, w_ff2):
    f = lambda a: np.ascontiguousarray(np.asarray(a, dtype=np.float32))
    x, c = f(x), f(c)
    shared = dict(
        w_ada=f(w_ada[0]), b_ada=f(b_ada[0]),
        b_ada_col=f(np.asarray(b_ada[0]).reshape(6, 8, 128).transpose(2, 0, 1)),
        w_pre_col=f(np.asarray(pre_mix_norm_w[0]).reshape(8, 128).T),
        w_preffn_col=f(np.asarray(pre_ffn_norm_w[0]).reshape(8, 128).T),
        w_post=f(post_mix_norm_w[0]), w_postffn=f(post_ffn_norm_w[0]),
        w_in=f(w_in[0]),
        dn_conv_col=f(np.asarray(dn_conv_w[0]).T.reshape(12, 128, 4).transpose(1, 0, 2)),
        dn_a_log=f(dn_a_log[0]), dn_dt_bias=f(dn_dt_bias[0]), dn_norm_w=f(dn_norm_w[0]),
        sc_conv_col=f(np.asarray(sc_conv_w[0]).T.reshape(4, 128, 3).transpose(1, 0, 2)),
        w_out=f(w_out[0]), w_ff1=f(w_ff1[0]), w_ff2=f(w_ff2[0]),
    )
    shared.update(_consts())
    maps = []
    for b in range(NCORES):
        m = dict(shared)
        m["x"] = x[b]
        m["ccol"] = f(c[b].reshape(8, 128).T)
        maps.append(m)
    return maps


def kernel(**inputs):
    nc, _ = build_program(dbg=False)
    in_maps = make_in_maps(**inputs)
    res = run_bass_kernel_spmd(nc, in_maps, core_ids=list(range(NCORES)))
    return np.stack([np.asarray(r["out"], dtype=np.float32) for r in res.results], axis=0)
```
